# Optimizing a Trainium2 kernel written in Bass

```python
import jax, jax.numpy as jnp
from jax import lax
import numpy as np

D_MODEL = 2048
BATCH = 1
SEQ = 16384
DEPTH = 2

EPS = 1e-6
NEG_INF = -1e30
BLOCK = 128
HEAD_DIM = 64
ROPE_DIM = HEAD_DIM // 4
ROPE_THETA = 500000.0
A_DILATED = ((128, 1), (512, 4), (2048, 16))
A_HEADS_PER_GROUP = 6
A_HEADS = len(A_DILATED) * A_HEADS_PER_GROUP
A_WIDTH = A_HEADS * HEAD_DIM
A_OUT = A_HEADS_PER_GROUP * HEAD_DIM
B_WINDOW = 128
B_Q_HEADS = 16
B_KV_HEADS = 4
B_REP = B_Q_HEADS // B_KV_HEADS
B_Q_WIDTH = B_Q_HEADS * HEAD_DIM
B_KV_WIDTH = B_KV_HEADS * HEAD_DIM
C_HEADS = 8
C_HEAD_DIM = 128
C_WIDTH = C_HEADS * C_HEAD_DIM
C_CHUNK = 128
C_ROT_THETA = 10000.0
D_FF = 4 * D_MODEL
IN_SIZES = (A_WIDTH, A_WIDTH, A_WIDTH,
            B_Q_WIDTH, B_KV_WIDTH, B_KV_WIDTH,
            C_WIDTH, C_WIDTH, C_WIDTH, C_WIDTH,
            D_MODEL, D_MODEL, D_MODEL)
D_IN = sum(IN_SIZES)

kernel_name = "hybrid_dilated_swa_sink_retention_gated_block"


def rms_norm(x, g):
    xf = x.astype(jnp.float32)
    y = xf * lax.rsqrt(jnp.mean(xf * xf, axis=-1, keepdims=True) + EPS)
    return (y * g.astype(jnp.float32)).astype(x.dtype)


def rotate(x, pos, rot_dim, theta):
    half = rot_dim // 2
    inv = theta ** (-jnp.arange(half, dtype=jnp.float32) / half)
    ang = pos.astype(jnp.float32)[:, None] * inv[None, :]
    cos = jnp.cos(ang)[None, :, None, :]
    sin = jnp.sin(ang)[None, :, None, :]
    xr = x[..., :rot_dim].astype(jnp.float32)
    x1, x2 = xr[..., :half], xr[..., half:]
    rot = jnp.concatenate([x1 * cos - x2 * sin, x2 * cos + x1 * sin], axis=-1).astype(x.dtype)
    return jnp.concatenate([rot, x[..., rot_dim:]], axis=-1)


def split_columns(p):
    outs, start = [], 0
    for size in IN_SIZES:
        outs.append(p[..., start:start + size])
        start += size
    return outs


def banded_attention(q, k, v, max_dist, sinks=None):
    n, L, g, r, dh = q.shape
    nb = -(-L // BLOCK)
    pad = nb * BLOCK - L
    if pad:
        q = jnp.pad(q, ((0, 0), (0, pad), (0, 0), (0, 0), (0, 0)))
        k = jnp.pad(k, ((0, 0), (0, pad), (0, 0), (0, 0)))
        v = jnp.pad(v, ((0, 0), (0, pad), (0, 0), (0, 0)))
    qb = q.reshape(n, nb, BLOCK, g, r, dh)
    kb = k.reshape(n, nb, BLOCK, g, dh)
    vb = v.reshape(n, nb, BLOCK, g, dh)

    def with_prev(t):
        prev = jnp.pad(t[:, :-1], ((0, 0), (1, 0), (0, 0), (0, 0), (0, 0)))
        return jnp.concatenate([prev, t], axis=2)

    kk, vv = with_prev(kb), with_prev(vb)
    s = jnp.einsum('nbqgrd,nbkgd->nbgrqk', qb, kk,
                   preferred_element_type=jnp.float32) * (dh ** -0.5)
    qi = jnp.arange(BLOCK)[:, None]
    kj = jnp.arange(2 * BLOCK)[None, :]
    dist = qi + BLOCK - kj
    band = (dist >= 0) & (dist <= max_dist)
    has_prev = (jnp.arange(nb) > 0)[:, None, None] | (kj >= BLOCK)[None]
    valid = band[None] & has_prev
    s = jnp.where(valid[None, :, None, None], s, NEG_INF)
    m = jnp.max(s, axis=-1, keepdims=True)
    if sinks is not None:
        sk = sinks.astype(jnp.float32).reshape(1, 1, g, r, 1, 1)
        m = jnp.maximum(m, sk)
    p = jnp.exp(s - m)
    denom = jnp.sum(p, axis=-1, keepdims=True)
    if sinks is not None:
        denom = denom + jnp.exp(sk - m)
    o = jnp.einsum('nbgrqk,nbkgd->nbqgrd', p, vv.astype(jnp.float32))
    denom_q = jnp.moveaxis(denom[..., 0], -1, 2)
    o = (o / denom_q[..., None]).reshape(n, nb * BLOCK, g, r, dh)[:, :L]
    lse = jnp.moveaxis((m + jnp.log(denom))[..., 0], -1, 2)
    lse = lse.reshape(n, nb * BLOCK, g, r)[:, :L]
    return o.astype(q.dtype), lse


def dilated_attention(q, k, v):
    b, s = q.shape[:2]
    hpg = A_HEADS_PER_GROUP
    outs, lses = [], []
    for gi, (window, dil) in enumerate(A_DILATED):
        sl = slice(gi * hpg, (gi + 1) * hpg)

        def strided(t):
            t = t.reshape(b, s // dil, dil, hpg, HEAD_DIM).transpose(0, 2, 1, 3, 4)
            return t.reshape(b * dil, s // dil, hpg, HEAD_DIM)

        o, lse = banded_attention(strided(q[:, :, sl])[:, :, :, None],
                                  strided(k[:, :, sl]), strided(v[:, :, sl]), window // dil)
        o = o[:, :, :, 0].reshape(b, dil, s // dil, hpg, HEAD_DIM).transpose(0, 2, 1, 3, 4)
        lse = lse[:, :, :, 0].reshape(b, dil, s // dil, hpg).transpose(0, 2, 1, 3)
        outs.append(o.reshape(b, s, hpg, HEAD_DIM))
        lses.append(lse.reshape(b, s, hpg))
    w = jax.nn.softmax(jnp.stack(lses, axis=0), axis=0)
    o = jnp.sum(w[..., None] * jnp.stack(outs, axis=0).astype(jnp.float32), axis=0)
    return o.astype(q.dtype)


def retention(q, k, v, pos):
    b, s, h, dk = q.shape
    dv = v.shape[-1]
    q = rotate(q, pos, dk, C_ROT_THETA)
    k = rotate(k, pos, dk, C_ROT_THETA) * (dk ** -0.5)
    log_g = jnp.log1p(-(2.0 ** (-5.0 - jnp.arange(h, dtype=jnp.float32))))
    n, c = s // C_CHUNK, C_CHUNK
    qc = q.reshape(b, n, c, h, dk)
    kc = k.reshape(b, n, c, h, dk)
    vc = v.reshape(b, n, c, h, dv).astype(jnp.float32)
    i = jnp.arange(c, dtype=jnp.float32)
    rel = i[:, None] - i[None, :]
    decay = jnp.where(rel >= 0, jnp.exp(log_g[:, None, None] * jnp.maximum(rel, 0.0)), 0.0)
    inner = jnp.einsum('bnihd,bnjhd->bnhij', qc, kc, preferred_element_type=jnp.float32) * decay
    inner = jnp.einsum('bnhij,bnjhe->bnihe', inner, vc)
    k_decay = jnp.exp(log_g[:, None] * (c - 1 - i)[None, :])
    kv = jnp.einsum('bnjhd,hj,bnjhe->bnhde', kc.astype(jnp.float32), k_decay, vc)
    chunk_decay = jnp.exp(log_g * c)[None, :, None, None]

    def step(state, kv_n):
        return chunk_decay * state + kv_n, state

    _, prev = lax.scan(step, jnp.zeros((b, h, dk, dv), jnp.float32), jnp.moveaxis(kv, 1, 0))
    prev = jnp.moveaxis(prev, 0, 1)
    q_decay = jnp.exp(log_g[None, :] * (i + 1.0)[:, None])
    cross = jnp.einsum('bnihd,bnhde->bnihe', qc.astype(jnp.float32), prev) * q_decay[None, None, :, :, None]
    return (inner + cross).reshape(b, s, h, dv)


def setup_inputs(seed: int = 0) -> dict:
    key = jax.random.key(seed)
    ks = jax.random.split(key, 18)
    f32 = jnp.float32

    def nrm(k, shape, scale):
        return jax.random.normal(k, shape, f32) * scale

    def gain(k, shape):
        return 1.0 + 0.02 * jax.random.normal(k, shape, f32)

    return {
        "x": nrm(ks[0], (BATCH, SEQ, D_MODEL), 1.0),
        "mix_norm": gain(ks[1], (DEPTH, D_MODEL)),
        "w_in": nrm(ks[2], (DEPTH, D_MODEL, D_IN), D_MODEL ** -0.5),
        "a_q_norm": gain(ks[3], (DEPTH, HEAD_DIM)),
        "a_k_norm": gain(ks[4], (DEPTH, HEAD_DIM)),
        "b_q_norm": gain(ks[5], (DEPTH, HEAD_DIM)),
        "b_k_norm": gain(ks[6], (DEPTH, HEAD_DIM)),
        "b_sinks": nrm(ks[7], (DEPTH, B_Q_HEADS), 0.5),
        "c_gn": gain(ks[8], (DEPTH, C_WIDTH)),
        "w_br_a": nrm(ks[9], (DEPTH, A_OUT, D_MODEL), A_OUT ** -0.5),
        "w_br_b": nrm(ks[10], (DEPTH, B_Q_WIDTH, D_MODEL), B_Q_WIDTH ** -0.5),
        "w_br_c": nrm(ks[11], (DEPTH, C_WIDTH, D_MODEL), C_WIDTH ** -0.5),
        "w_out": nrm(ks[12], (DEPTH, D_MODEL, D_MODEL), D_MODEL ** -0.5),
        "mlp_norm": gain(ks[13], (DEPTH, D_MODEL)),
        "w_up": nrm(ks[14], (DEPTH, D_MODEL, D_FF), D_MODEL ** -0.5),
        "w_down": nrm(ks[15], (DEPTH, D_FF, D_MODEL), D_FF ** -0.5),
    }


def reference(x, mix_norm, w_in, a_q_norm, a_k_norm, b_q_norm, b_k_norm, b_sinks, c_gn,
              w_br_a, w_br_b, w_br_c, w_out, mlp_norm, w_up, w_down):
    b, s, _ = x.shape
    pos = jnp.arange(s)
    for l in range(DEPTH):
        u = rms_norm(x, mix_norm[l])
        proj = jnp.einsum('bsd,de->bse', u, w_in[l])
        aq, ak, av, bq, bk, bv, cq, ck, cv, cg, ga, gb, gc = split_columns(proj)

        aq = rotate(rms_norm(aq.reshape(b, s, A_HEADS, HEAD_DIM), a_q_norm[l]), pos, ROPE_DIM, ROPE_THETA)
        ak = rotate(rms_norm(ak.reshape(b, s, A_HEADS, HEAD_DIM), a_k_norm[l]), pos, ROPE_DIM, ROPE_THETA)
        av = av.reshape(b, s, A_HEADS, HEAD_DIM)
        o_a = dilated_attention(aq, ak, av).reshape(b, s, A_OUT)

        bq = rotate(rms_norm(bq.reshape(b, s, B_Q_HEADS, HEAD_DIM), b_q_norm[l]), pos, ROPE_DIM, ROPE_THETA)
        bk = rotate(rms_norm(bk.reshape(b, s, B_KV_HEADS, HEAD_DIM), b_k_norm[l]), pos, ROPE_DIM, ROPE_THETA)
        bv = bv.reshape(b, s, B_KV_HEADS, HEAD_DIM)
        o_b, _ = banded_attention(bq.reshape(b, s, B_KV_HEADS, B_REP, HEAD_DIM), bk, bv,
                                  B_WINDOW - 1, sinks=b_sinks[l])
        o_b = o_b.reshape(b, s, B_Q_WIDTH)

        y = retention(cq.reshape(b, s, C_HEADS, C_HEAD_DIM), ck.reshape(b, s, C_HEADS, C_HEAD_DIM),
                      cv.reshape(b, s, C_HEADS, C_HEAD_DIM), pos)
        y = y * lax.rsqrt(jnp.mean(y * y, axis=-1, keepdims=True) + EPS)
        y = y.reshape(b, s, C_WIDTH) * c_gn[l].astype(jnp.float32)
        o_c = (jax.nn.silu(cg.astype(jnp.float32)) * y).astype(x.dtype)

        merged = (jax.nn.sigmoid(ga) * jnp.einsum('bse,ed->bsd', o_a, w_br_a[l])
                  + jax.nn.sigmoid(gb) * jnp.einsum('bse,ed->bsd', o_b, w_br_b[l])
                  + jax.nn.sigmoid(gc) * jnp.einsum('bse,ed->bsd', o_c, w_br_c[l]))
        x = x + jnp.einsum('bsd,de->bse', merged, w_out[l])

        hdn = jnp.einsum('bsd,df->bsf', rms_norm(x, mlp_norm[l]), w_up[l])
        hdn = jnp.square(jax.nn.relu(hdn))
        x = x + jnp.einsum('bsf,fd->bsd', hdn, w_down[l])
    return x
```

```python
import numpy as np
import ml_dtypes
import concourse.bass as bass
import concourse.mybir as mybir
from concourse.bass_utils import run_bass_kernel_spmd

F32 = mybir.dt.float32
BF16 = mybir.dt.bfloat16
AF = mybir.ActivationFunctionType
ALU = mybir.AluOpType
NPBF = ml_dtypes.bfloat16

TL = 2048
NCORE = 8
SEQ = 16384
EPS = 1e-6
SAME_ENGINE_SYNC = True
A_DIL = (1, 4, 16)

_uid = [0]


def uid():
    _uid[0] += 1
    return _uid[0]


class Stage:
    ENG = ("pe", "act", "dve", "pool", "sp")

    def __init__(self, nc):
        self.nc = nc
        self.ops = {e: [] for e in self.ENG}
        self.cnt = {e: 0 for e in self.ENG}
        self.dcnt = {}
        self.last_w = {}
        self.readers = {}
        self.seen = {e: {} for e in self.ENG}
        self.nsb = 0

    def sb(self, shape, dt, name="t"):
        return self.nc.alloc_sbuf_tensor(f"{name}_{uid()}", list(shape), dt)

    def ps(self, shape, dt=F32, name="p"):
        return self.nc.alloc_psum_tensor(f"{name}_{uid()}", list(shape), dt)

    PSUM_NAMES = ("psm", "ss", "aux_ss", "aux_rot", "psT", "psT_", "oacc", "sps", "bc", "ps_s", "ps_y", "pkv", "ps")

    def op(self, eng, fn, reads=(), writes=(), signal=True, dma=None):
        def is_ps(k):
            nm = k[0] if isinstance(k, tuple) else k
            return nm in self.PSUM_NAMES
        if eng != "pe":
            pr = [k for k in reads if is_ps(k)]
            if pr:
                reads = [k for k in reads if not is_ps(k)]
                writes = list(writes) + [k for k in pr if k not in writes]
        waits = {}

        def need(tok):
            if tok is None:
                return
            k, v = tok
            if k == eng and (eng in ("pe", "sp") or not SAME_ENGINE_SYNC):
                return
            if waits.get(k, 0) < v:
                waits[k] = v

        for b in reads:
            need(self.last_w.get(b))
        for b in writes:
            need(self.last_w.get(b))
            for t in self.readers.get(b, ()):
                need(t)
        wl = []
        for k, v in waits.items():
            if self.seen[eng].get(k, 0) >= v:
                continue
            self.seen[eng][k] = v
            wl.append((k, v))
        if dma is not None:
            key = ("dma", dma)
            self.dcnt[key] = self.dcnt.get(key, 0) + 16
            tok = (key, self.dcnt[key])
            inc = tok
        else:
            if signal:
                self.cnt[eng] += 1
                tok = (eng, self.cnt[eng])
                inc = tok
            else:
                tok = (eng, self.cnt[eng] + 1)
                inc = None
        self.ops[eng].append((wl, fn, inc))
        for b in writes:
            self.last_w[b] = tok
            self.readers[b] = []
        for b in reads:
            self.readers.setdefault(b, []).append(tok)

    def dma(self, q, out, in_, reads, writes, key):
        self.op(q, lambda e: e.dma_start(out=out, in_=in_), reads, writes, dma=key)

    def act(self, out, in_, func, reads, writes, **kw):
        self.op("act", lambda e: e.activation(out=out, in_=in_, func=func, **kw), reads, writes)

    def mm(self, out, lhsT, rhs, start, stop, reads, writes, signal=True):
        self.op("pe", lambda e: e.matmul(out, lhsT=lhsT, rhs=rhs, start=start, stop=stop), reads, writes,
                signal=signal)

    def tr(self, out, in_, ident, reads, writes):
        self.op("pe", lambda e: e.transpose(out, in_, ident), reads, writes)

    def tt(self, eng, out, in0, in1, op, reads, writes):
        self.op(eng, lambda e: e.tensor_tensor(out=out, in0=in0, in1=in1, op=op), reads, writes)

    def stt(self, eng, out, in0, scalar, in1, op0, op1, reads, writes):
        self.op(eng, lambda e: e.scalar_tensor_tensor(out=out, in0=in0, scalar=scalar, in1=in1, op0=op0, op1=op1),
                reads, writes)

    def ts(self, eng, out, in0, s1, op0, reads, writes, s2=None, op1=None):
        if op1 is None:
            self.op(eng, lambda e: e.tensor_scalar(out=out, in0=in0, scalar1=s1, scalar2=None, op0=op0), reads, writes)
        else:
            self.op(eng, lambda e: e.tensor_scalar(out=out, in0=in0, scalar1=s1, scalar2=s2, op0=op0, op1=op1),
                    reads, writes)

    def copy(self, eng, out, in_, reads, writes):
        self.op(eng, lambda e: e.tensor_copy(out=out, in_=in_), reads, writes)

    def recip(self, out, in_, reads, writes):
        self.op("dve", lambda e: e.reciprocal(out=out, in_=in_), reads, writes)

    def memset(self, eng, ap, val, writes):
        self.op(eng, lambda e: e.memset(ap, val), (), writes)

    def emit(self):
        nc = self.nc
        finals = [(k, v) for k, v in self.dcnt.items()]
        sems = {}
        for e in self.ENG:
            if self.cnt[e] > 0:
                sems[e] = nc.alloc_semaphore(f"s{e}{uid()}")
        for k in self.dcnt:
            sems[k] = nc.alloc_semaphore(f"sd{uid()}")
        ops = self.ops
        with nc.Block() as block:
            def mk(eng):
                def body(e):
                    for wl, fn, inc in ops[eng]:
                        for k, v in wl:
                            e.wait_ge(sems[k], v)
                        ins = fn(e)
                        if inc is not None:
                            ins.then_inc(sems[inc[0]], 16 if isinstance(inc[0], tuple) else 1)
                    if eng == "sp":
                        for k, v in finals:
                            e.wait_ge(sems[k], v)
                return body

            block.tensor(mk("pe"))
            block.scalar(mk("act"))
            block.vector(mk("dve"))
            block.gpsimd(mk("pool"))
            block.sync(mk("sp"))


class Prog:
    def __init__(self, ext_in, ext_out):
        self.nc = bass.Bass("TRN2", target_bir_lowering=False)
        self.ext_in = set(ext_in)
        self.ext_out = set(ext_out)
        self.t = {}
        self.used_in = {}
        self.used_out = {}

    def dram(self, name, shape, dt):
        if name in self.t:
            return self.t[name]
        if name in self.ext_in:
            kind = "ExternalInput"
            self.used_in[name] = (tuple(shape), dt)
        elif name in self.ext_out:
            kind = "ExternalOutput"
            self.used_out[name] = (tuple(shape), dt)
        else:
            kind = "Internal"
        ap = self.nc.dram_tensor(name, list(shape), dt, kind=kind).ap()
        self.t[name] = ap
        return ap


def slots(st, n, shape, dt, name):
    return [st.sb(shape, dt, name) for _ in range(n)]


def stage_norm(P, xname, gname, uname):
    nc = P.nc
    xT = P.dram(xname, [2048, TL], F32)
    g = P.dram(gname, [128, 16], F32)
    uT = P.dram(uname, [2048, TL], BF16)
    with nc.cleanup_on_exit():
        st = Stage(nc)
        xs = st.sb([128, 16, 512], F32, "xs")
        sq = slots(st, 3, [128, 512], F32, "sq")
        gt = st.sb([128, 16], F32, "gt")
        ones = st.sb([128, 128], F32, "ones")
        epst = st.sb([128, 1], F32, "eps")
        std = slots(st, 2, [128, 512], F32, "std")
        rstd = slots(st, 2, [128, 512], F32, "rstd")
        ub = slots(st, 4, [128, 512], BF16, "ub")
        ss = [st.ps([128, 512]) for _ in range(2)]
        st.dma("sp", gt[:], g, [], ["gt"], "gt")
        st.memset("pool", ones[:], 1.0, ["ones"])
        st.memset("pool", epst[:], EPS, ["eps"])
        k = 0
        ku = 0
        for tb in range(TL // 512):
            cs = slice(tb * 512, (tb + 1) * 512)
            for c in range(16):
                st.dma("sp", xs[:, c, :], xT[c * 128:(c + 1) * 128, cs], [], [("xs", c)], ("xs", c))
            for c in range(16):
                st.act(sq[k % 3][:], xs[:, c, :], AF.Square, [("xs", c)], [("sq", k % 3)])
                st.mm(ss[tb % 2][:], ones[:], sq[k % 3][:], c == 0, c == 15, ["ones", ("sq", k % 3)],
                      [("ss", tb % 2)])
                k += 1
            s2 = tb % 2
            st.act(std[s2][:], ss[s2][:], AF.Sqrt, [("ss", s2)], [("std", s2)], bias=EPS, scale=1.0 / 2048)
            st.recip(rstd[s2][:], std[s2][:], [("std", s2)], [("rstd", s2)])
            for c in range(16):
                u = ku % 4
                ku += 1
                st.stt("dve", ub[u][:], xs[:, c, :], gt[:, c:c + 1], rstd[s2][:], ALU.mult, ALU.mult,
                       [("xs", c), "gt", ("rstd", s2)], [("ub", u)])
                st.dma("sp", uT[c * 128:(c + 1) * 128, cs], ub[u][:], [("ub", u)], [], ("ub", u))
        st.emit()
        nc.all_engine_barrier()


def load_inT(st, in_ap, KC, T=TL, t0=0, name="in"):
    sbt = st.sb([128, KC, T], BF16, name)
    keys = []
    for kc in range(KC):
        key = (name, kc)
        st.dma("sp", sbt[:, kc, :], in_ap[kc * 128:(kc + 1) * 128, t0:t0 + T], [], [key], key)
        keys.append(key)
    return sbt, keys


def linF(st, in_sb, in_keys, KC, jobs, T=TL, nbank=4, conv_eng=("pool",)):
    nc = st.nc
    wst = slots(st, 3, [128, KC * 128], F32, "wst")
    wbf = slots(st, 2, [128, KC * 128], BF16, "wbf")
    ps = [st.ps([128, 512]) for _ in range(nbank)]
    pfx = uid()
    seq = [(j, n) for j, job in enumerate(jobs) for n in range(job[1])]

    def load(i):
        j, n = seq[i]
        s = i % 3
        st.dma("sp", wst[s][:], jobs[j][0][n], [], [("wst", pfx, s)], ("wst", pfx, s))

    def conv(i):
        s3 = i % 3
        s2 = i % 2
        ce = conv_eng[i % len(conv_eng)]
        if ce == "act":
            st.act(wbf[s2][:], wst[s3][:], AF.Copy, [("wst", pfx, s3)], [("wbf", pfx, s2)])
        else:
            st.copy(ce, wbf[s2][:], wst[s3][:], [("wst", pfx, s3)], [("wbf", pfx, s2)])

    pending = []

    def run_pending(flush):
        while True:
            todo = pending[:-1] if not flush else pending
            if not flush:
                for phl in reversed(todo):
                    phl.pop(0)()
                pending[:] = [p for p in pending if p]
                return
            if not pending:
                return
            for phl in reversed(list(pending)):
                phl.pop(0)()
            pending[:] = [p for p in pending if p]

    for i in range(min(3, len(seq))):
        load(i)
    conv(0)
    u = 0
    for i, (j, n) in enumerate(seq):
        if i + 3 < len(seq):
            load(i + 3)
        if i + 1 < len(seq):
            conv(i + 1)
        s2 = i % 2
        for tb in range(T // 512):
            b = u % nbank
            u += 1
            for kc in range(KC):
                st.mm(ps[b][:], wbf[s2][:, kc * 128:(kc + 1) * 128], in_sb[:, kc, tb * 512:(tb + 1) * 512],
                      kc == 0, kc == KC - 1, [("wbf", pfx, s2), in_keys[kc]], [("psm", pfx, b)],
                      signal=(kc == KC - 1))
            ph = jobs[j][2](n, tb, ps[b], ("psm", pfx, b))
            if ph:
                pending.append(list(ph))
            run_pending(False)
    run_pending(True)


def h_act(st, func, out_dram, row0=0, store_q="sp"):
    stg = slots(st, 2, [128, TL], BF16, "stg")
    pfx = uid()

    def h(n, tb, ps, pskey):
        sk = n % 2
        st.act(stg[sk][:, tb * 512:(tb + 1) * 512], ps[:], func, [pskey], [("stg", pfx, sk, tb)])
        if tb == TL // 512 - 1:
            st.dma(store_q, out_dram[row0 + n * 128: row0 + (n + 1) * 128, :], stg[sk][:],
                   [("stg", pfx, sk, t) for t in range(TL // 512)], [], ("stg", pfx, sk))

    return h


def h_relu2_sb(st, dst_sb, dst_keys):
    r = slots(st, 3, [128, 512], F32, "r")
    pfx = uid()
    c = [0]

    def h(n, tb, ps, pskey):
        k = c[0] % 3
        c[0] += 1
        st.act(r[k][:], ps[:], AF.Relu, [pskey], [("r", pfx, k)])
        st.tt("pool", dst_sb[:, n, tb * 512:(tb + 1) * 512], r[k][:], r[k][:], ALU.mult, [("r", pfx, k)],
              [(dst_keys, n, tb)])

    return h


def h_resid(st, x_in, x_out, xkey):
    if not hasattr(st, "_xo"):
        st._xo = slots(st, 2, [128, TL], F32, "xo")
        st._xn = slots(st, 2, [128, TL], F32, "xn")
        st._xc = [0]
    xo, xn, c = st._xo, st._xn, st._xc
    pfx = "rs"

    def h(n, tb, ps, pskey):
        if tb == 0:
            c[0] += 1
        k = c[0] % 2
        rs = slice(n * 128, (n + 1) * 128)
        cs = slice(tb * 512, (tb + 1) * 512)
        if tb == 0:
            st.dma("sp", xo[k][:], x_in[rs, :], [(xkey, n)], [("xo", pfx, k)], ("xo", pfx, k))
        st.tt("dve", xn[k][:, cs], ps[:], xo[k][:, cs], ALU.add, [pskey, ("xo", pfx, k)], [("xn", pfx, k, tb)])
        if tb == TL // 512 - 1:
            st.dma("sp", x_out[rs, :], xn[k][:], [("xn", pfx, k, t) for t in range(TL // 512)], [(xkey, n)],
                   ("xn", pfx, k))

    return h


class RopeCtx:
    def __init__(self, st, P, kind, core_tabs=True):
        nc = st.nc
        self.st = st
        self.kind = kind
        sfx = "ab" if kind == "ab" else "c"
        cos_d = P.dram("cos_" + sfx, [128, TL], F32)
        sin_d = P.dram("sin_" + sfx, [128, TL], F32)
        rot_d = P.dram("rot_" + sfx, [128, 128], BF16)
        self.C = st.sb([128, TL], F32, "cos")
        self.S = st.sb([128, TL], F32, "sin")
        self.R = st.sb([128, 128], BF16, "rot")
        st.dma("sp", self.C[:], cos_d, [], ["cos"], "cos")
        st.dma("sp", self.S[:], sin_d, [], ["sin"], "sin")
        st.dma("sp", self.R[:], rot_d, [], ["rot"], "rot")
        self.aux_rot = [st.ps([128, 512]) for _ in range(2)]
        if kind == "ab":
            bo_d = P.dram("bones", [128, 128], F32)
            self.bones = st.sb([128, 128], F32, "bones")
            st.dma("sp", self.bones[:], bo_d, [], ["bones"], "bones")
            self.aux_ss = [st.ps([128, 512]) for _ in range(2)]
            self.epst = st.sb([128, 1], F32, "eps")
            st.memset("pool", self.epst[:], EPS, ["eps"])
            self.sq = slots(st, 2, [128, 512], F32, "sq")
            self.std = slots(st, 2, [128, 512], F32, "std")
            self.rstd = slots(st, 2, [128, 512], F32, "rstd")
        self.qn = slots(st, 2, [128, 512], BF16, "qn")
        self.t1 = slots(st, 2, [128, 512], F32, "t1")
        self.t2 = slots(st, 2, [128, 512], F32, "t2")
        self.cnt = 0


def h_qk(ctx, gain_sb, gain_key, dil_of_chunk, out_dram, row0=0):
    st = ctx.st
    stg = slots(st, 2, [128, TL], BF16, "stgq")
    pfx = uid()

    def h(n, tb, ps, pskey):
        k = ctx.cnt % 2
        ctx.cnt += 1
        cs = slice(tb * 512, (tb + 1) * 512)

        def p0():
            st.act(ctx.sq[k][:], ps[:], AF.Square, [pskey], [("sq", k)])
            st.mm(ctx.aux_ss[k][:], ctx.bones[:], ctx.sq[k][:], True, True, ["bones", ("sq", k)], [("aux_ss", k)])

        def p1():
            st.act(ctx.std[k][:], ctx.aux_ss[k][:], AF.Sqrt, [("aux_ss", k)], [("std", k)],
                   bias=EPS, scale=1.0 / 64)
            st.recip(ctx.rstd[k][:], ctx.std[k][:], [("std", k)], [("rstd", k)])
            st.stt("dve", ctx.qn[k][:], ps[:], gain_sb, ctx.rstd[k][:], ALU.mult, ALU.mult,
                   [pskey, gain_key, ("rstd", k)], [("qn", k)])
            st.mm(ctx.aux_rot[k][:], ctx.R[:], ctx.qn[k][:], True, True, ["rot", ("qn", k)], [("aux_rot", k)])

        def p2():
            st.tt("pool", ctx.t1[k][:], ctx.qn[k][:], ctx.C[:, cs], ALU.mult, [("qn", k), "cos"], [("t1", k)])
            st.tt("dve", ctx.t2[k][:], ctx.aux_rot[k][:], ctx.S[:, cs], ALU.mult, [("aux_rot", k), "sin"],
                  [("t2", k)])
            d = dil_of_chunk(n)
            sk = n % 2
            w = 512 // d
            o_ap = stg[sk][:].rearrange("p (r j) -> p r j", r=d)[:, :, tb * w:(tb + 1) * w]
            a1 = ctx.t1[k][:].rearrange("p (j r) -> p r j", r=d)
            a2 = ctx.t2[k][:].rearrange("p (j r) -> p r j", r=d)
            st.tt("pool", o_ap, a1, a2, ALU.add, [("t1", k), ("t2", k)], [("stgq", pfx, sk, tb)])
            if tb == TL // 512 - 1:
                st.dma("sp", out_dram[row0 + n * 128: row0 + (n + 1) * 128, :], stg[sk][:],
                       [("stgq", pfx, sk, t) for t in range(TL // 512)], [], ("stgq", pfx, sk))

        return [p0, p1, p2]

    return h


def h_crot(ctx, out_dram, tok_out=None, kdec_sb=None, ident=None):
    st = ctx.st
    stg = slots(st, 2, [128, TL], BF16, "stgc")
    pfx = uid()
    if tok_out is not None:
        psT = [st.ps([128, 512], F32, "psT") for _ in range(2)]
        kt = slots(st, 2, [128, 16, 128], BF16, "kt")

    def h(n, tb, ps, pskey):
        k = ctx.cnt % 2
        ctx.cnt += 1
        cs = slice(tb * 512, (tb + 1) * 512)

        def p0():
            st.act(ctx.qn[k][:], ps[:], AF.Copy, [pskey], [("qn", k)])
            st.mm(ctx.aux_rot[k][:], ctx.R[:], ctx.qn[k][:], True, True, ["rot", ("qn", k)], [("aux_rot", k)])

        def p1():
            h_tail(n, tb, ps, pskey, k, cs)

        return [p0, p1]

    def h_tail(n, tb, ps, pskey, k, cs):
        st.tt("dve", ctx.t1[k][:], ps[:], ctx.C[:, cs], ALU.mult, [pskey, "cos"], [("t1", k)])
        st.tt("dve", ctx.t2[k][:], ctx.aux_rot[k][:], ctx.S[:, cs], ALU.mult, [("aux_rot", k), "sin"], [("t2", k)])
        sk = n % 2
        st.tt("pool", stg[sk][:, cs], ctx.t1[k][:], ctx.t2[k][:], ALU.add, [("t1", k), ("t2", k)],
              [("stgc", pfx, sk, tb)])
        if tb == TL // 512 - 1:
            allk = [("stgc", pfx, sk, t) for t in range(TL // 512)]
            st.dma("sp", out_dram[n * 128:(n + 1) * 128, :], stg[sk][:], allk, [], ("stgc", pfx, sk))
            if tok_out is not None:
                for blk in range(16):
                    pk = blk % 2
                    st.mm(psT[pk][:, 0:128], stg[sk][:, blk * 128:(blk + 1) * 128], ident[:], True, True,
                          allk + ["ident"], [("psT", pk)])
                    st.ts("dve", kt[sk][:, blk, :], psT[pk][:, 0:128], kdec_sb[:, n:n + 1], ALU.mult,
                          [("psT", pk), "kdec"], [("kt", pfx, sk, blk)])
                st.dma("sp", tok_out[n].rearrange("(b p) d -> p b d", p=128), kt[sk][:],
                       [("kt", pfx, sk, b) for b in range(16)], [], ("kt", pfx, sk))

    return h


def linT(st, in_sb, in_keys, wT_ap, c0, ncols, d, out_dram, col0, heads, ps_list, vst, wst, wv, pfx):
    for kc in range(16):
        s = kc % len(wst)
        st.dma("sp", wst[s][:, 0:ncols], wT_ap[:, kc, c0:c0 + ncols], [], [("wstT", s)], ("wstT", s))
        st.copy("pool", wv[:, kc, 0:ncols], wst[s][:, 0:ncols], [("wstT", s)], [("wv", kc)])
    nb = 16 // d
    for pb in range(16):
        r, b = pb // nb, pb % nb
        tsl = slice(b * 128 * d + r, b * 128 * d + r + 127 * d + 1, d)
        pbank = pb % len(ps_list)
        ps = ps_list[pbank]
        for kc in range(16):
            st.mm(ps[:, 0:ncols], in_sb[:, kc, tsl], wv[:, kc, 0:ncols], kc == 0, kc == 15,
                  [in_keys[kc], ("wv", kc)], [("psT_", pbank)], signal=(kc == 15))
        vs = pb % len(vst)
        if heads:
            st.act(vst[vs][:, 0:heads, 0:64], ps[:, 0:ncols].rearrange("p (h e) -> p h e", e=64), AF.Copy,
                   [("psT_", pbank)], [("vst", vs)])
            st.dma("sp", out_dram[pb * 128:(pb + 1) * 128, col0:col0 + heads * 65],
                   vst[vs][:, 0:heads, :].rearrange("p h e -> p (h e)"), [("vst", vs)], [], ("vst", vs))
        else:
            st.act(vst[vs][:, 0:ncols], ps[:, 0:ncols], AF.Copy, [("psT_", pbank)], [("vst", vs)])
            st.dma("sp", out_dram[pb * 128:(pb + 1) * 128, col0:col0 + ncols], vst[vs][:, 0:ncols],
                   [("vst", vs)], [], ("vst", vs))


def stage_qk_ab(P, l, which):
    nc = P.nc
    uT = P.dram(f"uT{l}", [2048, TL], BF16)
    na, nb_ = 9, (2 if which == "k" else 8)
    wa = P.dram(f"w{which}_a{l}", [na, 128, 2048], F32)
    wb = P.dram(f"w{which}_b{l}", [nb_, 128, 2048], F32)
    ga = P.dram(f"g{which}_a{l}", [128, 1], F32)
    gb = P.dram(f"g{which}_b{l}", [128, 1], F32)
    oa = P.dram(f"{which}T_A{l}", [1152, TL], BF16)
    ob = P.dram(f"{which}T_B{l}", [nb_ * 128, TL], BF16)
    with nc.cleanup_on_exit():
        st = Stage(nc)
        in_sb, in_keys = load_inT(st, uT, 16)
        ctx = RopeCtx(st, P, "ab")
        gat = st.sb([128, 1], F32, "ga")
        gbt = st.sb([128, 1], F32, "gb")
        st.dma("sp", gat[:], ga, [], ["ga"], "ga")
        st.dma("sp", gbt[:], gb, [], ["gb"], "gb")
        jobs = [
            (wa, na, h_qk(ctx, gat[:], "ga", lambda n: A_DIL[(2 * n) // 6], oa)),
            (wb, nb_, h_qk(ctx, gbt[:], "gb", lambda n: 1, ob)),
        ]
        linF(st, in_sb, in_keys, 16, jobs)
        st.emit()
        nc.all_engine_barrier()


def stage_c_rot(P, l, which):
    nc = P.nc
    uT = P.dram(f"uT{l}", [2048, TL], BF16)
    w = P.dram(f"w{which}_c{l}", [8, 128, 2048], F32)
    o = P.dram(f"{which}T_C{l}", [1024, TL], BF16)
    with nc.cleanup_on_exit():
        st = Stage(nc)
        in_sb, in_keys = load_inT(st, uT, 16)
        ctx = RopeCtx(st, P, "c")
        if which == "k":
            tok = P.dram(f"ktok_C{l}", [8, TL, 128], BF16)
            kdec_d = P.dram("kdec", [128, 8], F32)
            id_d = P.dram("ident", [128, 128], BF16)
            kdec = st.sb([128, 8], F32, "kdec")
            ident = st.sb([128, 128], BF16, "ident")
            st.dma("sp", kdec[:], kdec_d, [], ["kdec"], "kdec")
            st.dma("sp", ident[:], id_d, [], ["ident"], "ident")
            h = h_crot(ctx, o, tok, kdec, ident)
        else:
            h = h_crot(ctx, o)
        linF(st, in_sb, in_keys, 16, [(w, 8, h)])
        st.emit()
        nc.all_engine_barrier()


def stage_v(P, l):
    nc = P.nc
    uT = P.dram(f"uT{l}", [2048, TL], BF16)
    wva = P.dram(f"wv_a{l}", [128, 16, 1152], F32)
    wvb = P.dram(f"wv_b{l}", [128, 16, 256], F32)
    wvc = P.dram(f"wv_c{l}", [128, 16, 1024], F32)
    VA = P.dram(f"V_A{l}", [TL, 18 * 65], BF16)
    VB = P.dram(f"V_B{l}", [TL, 4 * 65], BF16)
    VC = P.dram(f"V_C{l}", [TL, 1024], BF16)
    with nc.cleanup_on_exit():
        st = Stage(nc)
        in_sb, in_keys = load_inT(st, uT, 16)
        ps_list = [st.ps([128, 512]) for _ in range(4)]
        vst = slots(st, 3, [128, 6, 65], BF16, "vst")
        vstc = slots(st, 3, [128, 512], BF16, "vstc")
        wst = slots(st, 3, [128, 512], F32, "wstT")
        wv = st.sb([128, 16, 512], BF16, "wv")
        for i in range(3):
            st.memset("pool", vst[i][:], 1.0, [("vst", i)])
        pfx = uid()
        for g in range(3):
            linT(st, in_sb, in_keys, wva, g * 384, 384, A_DIL[g], VA, g * 390, 6, ps_list, vst, wst, wv, pfx)
        linT(st, in_sb, in_keys, wvb, 0, 256, 1, VB, 0, 4, ps_list, vst, wst, wv, pfx)
        for half in range(2):
            linT(st, in_sb, in_keys, wvc, half * 512, 512, 1, VC, half * 512, 0, ps_list, vstc, wst, wv, pfx)
        st.emit()
        nc.all_engine_barrier()


def stage_gates(P, l):
    nc = P.nc
    uT = P.dram(f"uT{l}", [2048, TL], BF16)
    wcg = P.dram(f"wg_c{l}", [8, 128, 2048], F32)
    wg = P.dram(f"wg_m{l}", [48, 128, 2048], F32)
    scg = P.dram(f"scgT{l}", [1024, TL], BF16)
    sg = P.dram(f"sgT{l}", [6144, TL], BF16)
    with nc.cleanup_on_exit():
        st = Stage(nc)
        in_sb, in_keys = load_inT(st, uT, 16)
        jobs = [(wg, 48, h_act(st, AF.Sigmoid, sg)), (wcg, 8, h_act(st, AF.Silu, scg))]
        linF(st, in_sb, in_keys, 16, jobs)
        st.emit()
        nc.all_engine_barrier()


def stage_attn(P, l):
    nc = P.nc
    qA = P.dram(f"qT_A{l}", [1152, TL], BF16)
    kA = P.dram(f"kT_A{l}", [1152, TL], BF16)
    khA = P.dram(f"khalo_A{l}", [1152, TL], BF16)
    VA = P.dram(f"V_A{l}", [TL, 1170], BF16)
    VhA = P.dram(f"vhalo_A{l}", [TL, 1170], BF16)
    qB = P.dram(f"qT_B{l}", [1024, TL], BF16)
    kB = P.dram(f"kT_B{l}", [256, TL], BF16)
    khB = P.dram(f"khalo_B{l}", [256, 128], BF16)
    VB = P.dram(f"V_B{l}", [TL, 260], BF16)
    VhB = P.dram(f"vhalo_B{l}", [128, 260], BF16)
    masks_d = P.dram("masks", [128, 4, 256], BF16)
    sink_d = P.dram(f"sinks{l}", [128, 16], F32)
    sel_d = P.dram("sel65", [65, 64], F32)
    oA = P.dram(f"oT_A{l}", [384, TL], BF16)
    oB = P.dram(f"oT_B{l}", [1024, TL], BF16)
    with nc.cleanup_on_exit():
        st = Stage(nc)
        Vsb = st.sb([128, 16, 1170], BF16, "Vsb")
        Vh = st.sb([128, 16, 1170], BF16, "Vh")
        VBs = st.sb([128, 16, 260], BF16, "VBs")
        VBh = st.sb([128, 260], BF16, "VBh")
        masks = st.sb([128, 4, 256], BF16, "masks")
        sink = st.sb([128, 16], F32, "sink")
        esink = st.sb([128, 16], F32, "esink")
        sel = st.sb([65, 64], F32, "sel")
        for b4 in range(4):
            rr = slice(b4 * 512, (b4 + 1) * 512)
            bb = slice(b4 * 4, (b4 + 1) * 4)
            st.dma("sp", Vsb[:, bb, :], VA[rr, :].rearrange("(b p) e -> p b e", p=128), [], [("Vsb", b4)], ("Vsb", b4))
            st.dma("sp", Vh[:, bb, :], VhA[rr, :].rearrange("(b p) e -> p b e", p=128), [], [("Vh", b4)], ("Vh", b4))
            st.dma("sp", VBs[:, bb, :], VB[rr, :].rearrange("(b p) e -> p b e", p=128), [], [("VBs", b4)], ("VBs", b4))
        st.dma("sp", VBh[:], VhB, [], ["VBh"], "VBh")
        st.dma("sp", masks[:], masks_d, [], ["masks"], "masks")
        st.dma("sp", sink[:], sink_d, [], ["sink"], "sink")
        st.dma("sp", sel[:], sel_d, [], ["sel"], "sel")
        st.act(esink[:], sink[:], AF.Exp, ["sink"], ["esink"])
        NQ = 3
        qs = slots(st, NQ, [64, TL], BF16, "qs")
        ks = slots(st, NQ, [64, TL], BF16, "ks")
        khs = slots(st, NQ, [64, TL], BF16, "khs")
        oacc = [st.ps([128, 512]) for _ in range(4)]
        sps_t = [st.ps([128, 512]) for _ in range(3)]
        sps = [sps_t[i][:, 0:256] for i in range(3)]
        bc = st.ps([128, 512])
        ex = slots(st, 4, [128, 256], BF16, "ex")
        pm = slots(st, 4, [128, 256], BF16, "pm")
        osb = slots(st, 2, [65, 512], F32, "osb")
        den = slots(st, 2, [64, 512], F32, "den")
        ostg = slots(st, 2, [64, TL], BF16, "ostg")

        units = []
        state = {"hq": 0, "slot_started": None}

        def add_head(q_rows, k_rows, kh_rows, kh_cols, d, Vt, Vht, vcol0, mask_n, mask_f, is_B, hs):
            s = hs % NQ
            st.dma("sp", qs[s][:], q_rows, [], [("qs", s)], ("qs", s))
            st.dma("sp", ks[s][:], k_rows, [], [("ks", s)], ("ks", s))
            st.dma("sp", khs[s][:, 0:kh_cols], kh_rows, [], [("khs", s)], ("khs", s))
            nb = 16 // d
            for r in range(d):
                for b in range(nb):
                    pb = r * nb + b
                    c0 = pb * 128
                    uu = dict(q=qs[s][:, c0:c0 + 128], kown=ks[s][:, c0:c0 + 128], s=s, pb=pb, d=d, r=r, b=b)
                    if b > 0:
                        uu["kprev"] = ks[s][:, c0 - 128:c0]
                        uu["kpk"] = ("ks", s)
                        uu["vprev"] = Vt[:, pb - 1, vcol0:vcol0 + 65]
                        uu["vpk"] = (("VBs" if is_B else "Vsb"), (pb - 1) // 4)
                        uu["mask"] = mask_n
                    else:
                        uu["kprev"] = khs[s][:, r * 128:(r + 1) * 128]
                        uu["kpk"] = ("khs", s)
                        if is_B:
                            uu["vprev"] = Vht[:, vcol0:vcol0 + 65]
                            uu["vpk"] = "VBh"
                        else:
                            uu["vprev"] = Vht[:, r, vcol0:vcol0 + 65]
                            uu["vpk"] = ("Vh", r // 4)
                        uu["mask"] = mask_f
                    uu["vown"] = Vt[:, pb, vcol0:vcol0 + 65]
                    uu["vok"] = (("VBs" if is_B else "Vsb"), pb // 4)
                    outs = []
                    if d == 1:
                        outs.append((b // 4, slice((b % 4) * 128, (b % 4) * 128 + 128), slice(0, 128)))
                    elif d == 4:
                        outs.append((b, slice(r, 512, 4), slice(0, 128)))
                    else:
                        for k4 in range(4):
                            outs.append((k4, slice(r, 512, 16), slice(k4 * 32, k4 * 32 + 32)))
                    uu["outs"] = outs
                    units.append(uu)

        def emit_S(i, uu):
            sl = i % 4
            s2_ = i % 3
            st.mm(sps[s2_][:, 0:128], uu["kprev"], uu["q"], True, True, [uu["kpk"], ("qs", uu["s"])],
                  [("sps", s2_)])
            st.mm(sps[s2_][:, 128:256], uu["kown"], uu["q"], True, True, [("ks", uu["s"]), ("qs", uu["s"])],
                  [("sps", s2_)])
            st.act(ex[sl][:], sps[s2_], AF.Exp, [("sps", s2_)], [("ex", sl)], scale=0.125)
            st.tt("pool", pm[sl][:], ex[sl][:], masks[:, uu["mask"], :], ALU.mult, [("ex", sl), "masks"],
                  [("pm", sl)])

        def emit_PV(i, uu, started):
            sl = i % 4
            for half, (vap, vk) in enumerate(((uu["vprev"], uu["vpk"]), (uu["vown"], uu["vok"]))):
                for (bank, osl, rsl) in uu["outs"]:
                    first = not started[bank]
                    started[bank] = True
                    rhs = pm[sl][:, half * 128 + rsl.start: half * 128 + rsl.stop]
                    st.mm(oacc[bank][0:65, osl], vap, rhs, first, False, [vk, ("pm", sl)], [("oacc", bank)])

        def run_units(started):
            LAG = 2
            n = len(units)
            for i in range(n + LAG):
                if i < n:
                    emit_S(state["hq"] + i, units[i])
                if i >= LAG:
                    emit_PV(state["hq"] + i - LAG, units[i - LAG], started)
            state["hq"] += n
            units.clear()

        def normalize(out_rows, sink_col, oslot):
            for bank in range(4):
                k = bank % 2
                st.act(osb[k][:], oacc[bank][0:65, :], AF.Copy, [("oacc", bank)], [("osb", k)])
                st.mm(bc[0:64, :], sel[:], osb[k][:], True, True, ["sel", ("osb", k)], ["bc"])
                if sink_col is not None:
                    st.ts("dve", den[k][:], bc[0:64, :], esink[0:64, sink_col:sink_col + 1], ALU.add,
                          ["bc", "esink"], [("den", k)])
                    st.recip(den[k][:], den[k][:], [("den", k)], [("den", k)])
                else:
                    st.recip(den[k][:], bc[0:64, :], ["bc"], [("den", k)])
                st.tt("dve", ostg[oslot][:, bank * 512:(bank + 1) * 512], osb[k][0:64, :], den[k][:], ALU.mult,
                      [("osb", k), ("den", k)], [("ostg", oslot, bank)])
            st.dma("sp", out_rows, ostg[oslot][:], [("ostg", oslot, b) for b in range(4)], [], ("ostg", oslot))

        hs = 0
        for h in range(6):
            started = [False] * 4
            for g in range(3):
                hd = g * 6 + h
                d = A_DIL[g]
                add_head(qA[hd * 64:(hd + 1) * 64, :], kA[hd * 64:(hd + 1) * 64, :],
                         khA[hd * 64:(hd + 1) * 64, 0:d * 128], d * 128, d, Vsb, Vh, hd * 65, 0, 1, False, hs)
                hs += 1
                run_units(started)
            normalize(oA[h * 64:(h + 1) * 64, :], None, h % 2)
        for qh in range(16):
            kvh = qh // 4
            started = [False] * 4
            add_head(qB[qh * 64:(qh + 1) * 64, :], kB[kvh * 64:(kvh + 1) * 64, :],
                     khB[kvh * 64:(kvh + 1) * 64, :], 128, 1, VBs, VBh, kvh * 65, 2, 3, True, hs)
            hs += 1
            run_units(started)
            normalize(oB[qh * 64:(qh + 1) * 64, :], qh, qh % 2)
        st.emit()
        nc.all_engine_barrier()


GAMMA = [1.0 - 2.0 ** (-5.0 - h) for h in range(8)]


def stage_ret(P, l, state_only):
    nc = P.nc
    ktok = P.dram(f"ktok_C{l}", [8, TL, 128], BF16)
    VC = P.dram(f"V_C{l}", [TL, 1024], BF16)
    with nc.cleanup_on_exit():
        st = Stage(nc)
        vt = slots(st, 4, [128, 16, 128], BF16, "vt")
        ktk = slots(st, 4, [128, 16, 128], BF16, "ktk")
        S = slots(st, 4, [128, 128], F32, "S")
        Sb = slots(st, 4, [128, 128], BF16, "Sb")
        pkv_t = [st.ps([128, 512]) for _ in range(2)]
        if state_only:
            E = P.dram(f"E{l}", [8, 128, 128], F32)
        else:
            qT = P.dram(f"qT_C{l}", [1024, TL], BF16)
            kT = P.dram(f"kT_C{l}", [1024, TL], BF16)
            scg = P.dram(f"scgT{l}", [1024, TL], BF16)
            Eall = P.dram(f"Eall{l}", [NCORE, 8, 128, 128], F32)
            coef_d = P.dram("ecoef", [128, NCORE * 8], F32)
            dT_d = P.dram("decT", [128, 8, 128], F32)
            qdec_d = P.dram("qdec", [128, 8, 128], F32)
            gn_d = P.dram(f"cgn{l}", [128, 8], F32)
            id_d = P.dram("ident", [128, 128], BF16)
            oC = P.dram(f"oT_C{l}", [1024, TL], BF16)
            coef = st.sb([128, NCORE * 8], F32, "coef")
            dT = st.sb([128, 8, 128], F32, "dT")
            qdec = st.sb([128, 8, 128], F32, "qdec")
            gn = st.sb([128, 8], F32, "gn")
            ident = st.sb([128, 128], BF16, "ident")
            epst = st.sb([128, 1], F32, "eps")
            st.memset("pool", epst[:], EPS, ["eps"])
            for t, dd, kk in ((coef, coef_d, "coef"), (dT, dT_d, "dT"), (qdec, qdec_d, "qdec"), (gn, gn_d, "gn"),
                              (ident, id_d, "ident")):
                st.dma("sp", t[:], dd, [], [kk], kk)
            qs = slots(st, 4, [128, TL], BF16, "qs")
            ks = slots(st, 4, [128, TL], BF16, "ks")
            qd = slots(st, 4, [128, TL], BF16, "qd")
            sg = slots(st, 4, [128, TL], BF16, "sg")
            Ein = slots(st, 2, [128, 128], F32, "Ein")
            ps_s = [st.ps([128, 512]) for _ in range(2)]
            ps_y = [st.ps([128, 512]) for _ in range(2)]
            psT = [st.ps([128, 512]) for _ in range(2)]
            pT = slots(st, 4, [128, 128], BF16, "pT")
            junk = slots(st, 4, [128, 128], F32, "junk")
            ssq = slots(st, 4, [128, 1], F32, "ssq")
            sd = slots(st, 4, [128, 1], F32, "sd")
            rs = slots(st, 4, [128, 1], F32, "rs")
            yn = slots(st, 4, [128, 128], BF16, "yn")
            ostg = slots(st, 4, [128, TL], BF16, "ostg")
        itc = [0]

        def setup(h):
            s = h % 4
            hh = h % 2
            st.dma("sp", vt[s][:], VC[:, h * 128:(h + 1) * 128].rearrange("(b p) e -> p b e", p=128), [],
                   [("vt", s)], ("vt", s))
            st.dma("sp", ktk[s][:], ktok[h].rearrange("(b p) e -> p b e", p=128), [], [("ktk", s)], ("ktk", s))
            c = dict(h=h, s=s, hh=hh, cur=0, g128=float(GAMMA[h] ** 128))
            S0 = S[hh * 2]
            k0 = ("S", hh, 0)
            if state_only:
                st.memset("dve", S0[:], 0.0, [k0])
            else:
                rows = slice(h * 128, (h + 1) * 128)
                st.dma("sp", qs[s][:], qT[rows, :], [], [("qs", s)], ("qs", s))
                st.dma("sp", ks[s][:], kT[rows, :], [], [("ks", s)], ("ks", s))
                st.dma("sp", sg[s][:], scg[rows, :], [], [("sg", s)], ("sg", s))
                st.tt("dve", qd[s][:].rearrange("p (n i) -> p n i", i=128),
                      qs[s][:].rearrange("p (n i) -> p n i", i=128),
                      qdec[:, h, :].unsqueeze(1).broadcast_to([128, 16, 128]), ALU.mult,
                      [("qs", s), "qdec"], [("qd", s)])
                st.memset("dve", S0[:], 0.0, [k0])
                for cc in range(NCORE):
                    e2 = cc % 2
                    st.dma("sp", Ein[e2][:], Eall[cc, h], [], [("Ein", e2)], ("Ein", e2))
                    col = cc * 8 + h
                    st.stt("dve", S0[:], Ein[e2][:], coef[:, col:col + 1], S0[:], ALU.mult, ALU.add,
                           [("Ein", e2), "coef", k0], [k0])
                st.act(Sb[hh * 2][:], S0[:], AF.Copy, [k0], [("Sb", hh, 0)])
            return c

        def body(c, n):
            h, s, hh, cur = c["h"], c["s"], c["hh"], c["cur"]
            it = itc[0]
            itc[0] += 1
            cs = slice(n * 128, (n + 1) * 128)
            if not state_only:
                i4 = it % 2
                i3 = it % 4
                st.mm(ps_s[i4][:, 0:128], ks[s][:, cs], qs[s][:, cs], True, True, [("ks", s), ("qs", s)],
                      [("ps_s", i4)])
                st.tt("dve", pT[i3][:], ps_s[i4][:, 0:128], dT[:, h, :], ALU.mult, [("ps_s", i4), "dT"],
                      [("pT", i3)])
                st.mm(ps_y[i4][:, 0:128], pT[i3][:], vt[s][:, n, :], True, False, [("pT", i3), ("vt", s)],
                      [("ps_y", i4)])
                st.mm(ps_y[i4][:, 0:128], qd[s][:, cs], Sb[hh * 2 + cur][:], False, True,
                      [("qd", s), ("Sb", hh, cur)], [("ps_y", i4)])
                st.act(junk[i3][:], ps_y[i4][:, 0:128], AF.Square, [("ps_y", i4)], [("junk", i3)])
                st.op("dve", (lambda o, i_: (lambda e: e.reduce_sum(out=o, in_=i_, axis=mybir.AxisListType.X)))(
                    ssq[i3][:], junk[i3][:]), [("junk", i3)], [("ssq", i3)])
                st.act(sd[i3][:], ssq[i3][:], AF.Sqrt, [("ssq", i3)], [("sd", i3)], bias=EPS, scale=1.0 / 128)
                st.recip(rs[i3][:], sd[i3][:], [("sd", i3)], [("rs", i3)])
                st.ts("dve", yn[i3][:], ps_y[i4][:, 0:128], rs[i3][:], ALU.mult, [("ps_y", i4), ("rs", i3)],
                      [("yn", i3)])
                st.mm(psT[i4][:, 0:128], yn[i3][:], ident[:], True, True, [("yn", i3), "ident"], [("psT", i4)])
                st.stt("dve", ostg[s][:, cs], psT[i4][:, 0:128], gn[:, h:h + 1], sg[s][:, cs], ALU.mult, ALU.mult,
                       [("psT", i4), "gn", ("sg", s)], [("ostg", s, n)])
            if n < 15 or state_only:
                k4 = it % 2
                st.mm(pkv_t[k4][:, 0:128], ktk[s][:, n, :], vt[s][:, n, :], True, True, [("ktk", s), ("vt", s)],
                      [("pkv", k4)])
                nxt = 1 - cur
                st.stt("dve", S[hh * 2 + nxt][:], S[hh * 2 + cur][:], c["g128"], pkv_t[k4][:, 0:128], ALU.mult,
                       ALU.add, [("S", hh, cur), ("pkv", k4)], [("S", hh, nxt)])
                if not state_only:
                    st.act(Sb[hh * 2 + nxt][:], S[hh * 2 + nxt][:], AF.Copy, [("S", hh, nxt)], [("Sb", hh, nxt)])
                c["cur"] = nxt

        def finish(c):
            h, s, hh, cur = c["h"], c["s"], c["hh"], c["cur"]
            if state_only:
                st.dma("sp", E[h], S[hh * 2 + cur][:], [("S", hh, cur)], [], ("Sout", hh))
            else:
                st.dma("sp", oC[h * 128:(h + 1) * 128, :], ostg[s][:], [("ostg", s, n) for n in range(16)], [],
                       ("ostg", s))

        for p in range(4):
            ctxs = [setup(2 * p), setup(2 * p + 1)]
            for n in range(16):
                for c in ctxs:
                    body(c, n)
            for c in ctxs:
                finish(c)
        st.emit()
        nc.all_engine_barrier()


def stage_merge(P, l):
    nc = P.nc
    oA = P.dram(f"oT_A{l}", [384, TL], BF16)
    oB = P.dram(f"oT_B{l}", [1024, TL], BF16)
    oC = P.dram(f"oT_C{l}", [1024, TL], BF16)
    sgd = P.dram(f"sgT{l}", [6144, TL], BF16)
    wbr = P.dram(f"w_br{l}", [16, 128, 19 * 128], F32)
    mT = P.dram(f"mT{l}", [2048, TL], BF16)
    with nc.cleanup_on_exit():
        st = Stage(nc)
        in_sb = st.sb([128, 19, TL], BF16, "oin")
        in_keys = []
        kc = 0
        for src, nch in ((oA, 3), (oB, 8), (oC, 8)):
            for c in range(nch):
                key = ("oin", kc)
                st.dma("sp", in_sb[:, kc, :], src[c * 128:(c + 1) * 128, :], [], [key], key)
                in_keys.append(key)
                kc += 1
        wst = slots(st, 3, [128, 19 * 128], F32, "wst")
        wbf = slots(st, 2, [128, 19 * 128], BF16, "wbf")
        sgs = slots(st, 2, [128, 3, TL], BF16, "sgs")
        ps = [st.ps([128, 512]) for _ in range(6)]
        m1 = slots(st, 2, [128, 512], F32, "m1")
        m2 = slots(st, 2, [128, 512], F32, "m2")
        m3 = slots(st, 2, [128, 512], F32, "m3")
        m4 = slots(st, 2, [128, 512], F32, "m4")
        stg = slots(st, 2, [128, TL], BF16, "stg")
        groups = ((0, 3), (3, 11), (11, 19))

        def load(n):
            s = n % 3
            st.dma("sp", wst[s][:], wbr[n], [], [("wst", s)], ("wst", s))

        def conv(n):
            st.copy("pool", wbf[n % 2][:], wst[n % 3][:], [("wst", n % 3)], [("wbf", n % 2)])

        def load_sg(n):
            for b3 in range(3):
                st.dma("sp", sgs[n % 2][:, b3, :], sgd[b3 * 2048 + n * 128: b3 * 2048 + (n + 1) * 128, :], [],
                       [("sgs", n % 2, b3)], ("sgs", n % 2, b3))

        load(0)
        load(1)
        load(2)
        conv(0)
        load_sg(0)
        u = 0
        for n in range(16):
            if n + 3 < 16:
                load(n + 3)
            if n + 1 < 16:
                conv(n + 1)
            s2 = n % 2
            if n + 1 < 16:
                load_sg(n + 1)
            for tb in range(4):
                cs = slice(tb * 512, (tb + 1) * 512)
                pb = (u % 2) * 3
                k = u % 2
                u += 1
                for b3, (k0, k1) in enumerate(groups):
                    for kc in range(k0, k1):
                        st.mm(ps[pb + b3][:], wbf[s2][:, kc * 128:(kc + 1) * 128], in_sb[:, kc, cs], kc == k0,
                              kc == k1 - 1, [("wbf", s2), in_keys[kc]], [("ps", pb + b3)], signal=(kc == k1 - 1))
                st.tt("dve", m1[k][:], ps[pb][:], sgs[s2][:, 0, cs], ALU.mult, [("ps", pb), ("sgs", s2, 0)],
                      [("m1", k)])
                st.tt("dve", m2[k][:], ps[pb + 1][:], sgs[s2][:, 1, cs], ALU.mult, [("ps", pb + 1), ("sgs", s2, 1)],
                      [("m2", k)])
                st.tt("dve", m3[k][:], ps[pb + 2][:], sgs[s2][:, 2, cs], ALU.mult, [("ps", pb + 2), ("sgs", s2, 2)],
                      [("m3", k)])
                st.tt("pool", m4[k][:], m1[k][:], m2[k][:], ALU.add, [("m1", k), ("m2", k)], [("m4", k)])
                st.tt("pool", stg[s2][:, cs], m4[k][:], m3[k][:], ALU.add, [("m4", k), ("m3", k)],
                      [("stg", s2, tb)])
            st.dma("sp", mT[n * 128:(n + 1) * 128, :], stg[s2][:], [("stg", s2, t) for t in range(4)], [],
                   ("stg", s2))
        st.emit()
        nc.all_engine_barrier()


def stage_wout(P, l, xin, xout):
    nc = P.nc
    mT = P.dram(f"mT{l}", [2048, TL], BF16)
    w = P.dram(f"w_out{l}", [16, 128, 2048], F32)
    xi = P.dram(xin, [2048, TL], F32)
    xo = P.dram(xout, [2048, TL], F32)
    with nc.cleanup_on_exit():
        st = Stage(nc)
        in_sb, in_keys = load_inT(st, mT, 16)
        linF(st, in_sb, in_keys, 16, [(w, 16, h_resid(st, xi, xo, "xres"))])
        st.emit()
        nc.all_engine_barrier()


def stage_mlp(P, l, uname, xin, xout):
    nc = P.nc
    u2 = P.dram(uname, [2048, TL], BF16)
    wu = P.dram(f"w_up{l}", [64, 128, 2048], F32)
    wd = P.dram(f"w_dn{l}", [4, 16, 128, 2048], F32)
    xi = P.dram(xin, [2048, TL], F32)
    xo = P.dram(xout, [2048, TL], F32)
    with nc.cleanup_on_exit():
        st = Stage(nc)
        in_sb, in_keys = load_inT(st, u2, 16)
        hq = st.sb([128, 16, TL], BF16, "hq")
        wst = slots(st, 3, [128, 2048], F32, "wst")
        wbf = slots(st, 2, [128, 2048], BF16, "wbf")
        ps = [st.ps([128, 512]) for _ in range(6)]
        hu = h_relu2_sb(st, hq, "hq")
        hd0 = h_resid(st, xi, xo, "xres")
        hd = h_resid(st, xo, xo, "xres") if xin != xout else hd0
        seq = []
        for q in range(4):
            for n in range(16):
                seq.append(("u", q, n))
            for n in range(16):
                seq.append(("d", q, n))

        def load(i):
            kind, q, n = seq[i]
            s = i % 3
            src = wu[q * 16 + n] if kind == "u" else wd[q, n]
            st.dma("sp", wst[s][:], src, [], [("wst", s)], ("wst", s))

        def conv(i):
            st.copy("pool", wbf[i % 2][:], wst[i % 3][:], [("wst", i % 3)], [("wbf", i % 2)])

        load(0)
        load(1)
        load(2)
        conv(0)
        u = 0
        for i, (kind, q, n) in enumerate(seq):
            if i + 3 < len(seq):
                load(i + 3)
            if i + 1 < len(seq):
                conv(i + 1)
            s2 = i % 2
            for tb in range(4):
                b = u % 6
                u += 1
                cs = slice(tb * 512, (tb + 1) * 512)
                for kc in range(16):
                    if kind == "u":
                        rhs, rk = in_sb[:, kc, cs], [in_keys[kc]]
                    else:
                        rhs, rk = hq[:, kc, cs], [("hq", kc, tb)]
                    st.mm(ps[b][:], wbf[s2][:, kc * 128:(kc + 1) * 128], rhs, kc == 0, kc == 15,
                          [("wbf", s2)] + rk, [("ps", b)], signal=(kc == 15))
                if kind == "u":
                    hu(n, tb, ps[b], ("ps", b))
                else:
                    (hd0 if q == 0 else hd)(n, tb, ps[b], ("ps", b))
        st.emit()
        nc.all_engine_barrier()


IN_SIZES = (1152, 1152, 1152, 1024, 256, 256, 1024, 1024, 1024, 1024, 2048, 2048, 2048)
OFF = np.concatenate([[0], np.cumsum(IN_SIZES)])


def tile_w(w):
    K, N = w.shape
    return np.ascontiguousarray(
        w.reshape(K // 128, 128, N // 128, 128).transpose(2, 1, 0, 3).reshape(N // 128, 128, K))


def tile_wT(w):
    K, N = w.shape
    return np.ascontiguousarray(w.reshape(K // 128, 128, N).transpose(1, 0, 2))


def rep128(v):
    return np.ascontiguousarray(np.broadcast_to(np.asarray(v, np.float32)[None, :], (128, len(v))))


def core_consts(c):
    f32 = np.float32
    pos = (np.arange(TL) + c * TL).astype(f32)
    out = {}
    inv = (f32(500000.0) ** (-(np.arange(8, dtype=f32) / f32(8)))).astype(f32)
    ang = (pos[None, :] * inv[:, None]).astype(f32)
    cos_ab = np.ones((128, TL), f32)
    sin_ab = np.zeros((128, TL), f32)
    rot_ab = np.zeros((128, 128), f32)
    for p in range(128):
        i = p % 64
        if i < 16:
            cos_ab[p] = np.cos(ang[i % 8])
            sin_ab[p] = np.sin(ang[i % 8])
            if i < 8:
                rot_ab[p + 8, p] = -1.0
            else:
                rot_ab[p - 8, p] = 1.0
    out["cos_ab"], out["sin_ab"], out["rot_ab"] = cos_ab, sin_ab, rot_ab.astype(NPBF)
    invc = (f32(10000.0) ** (-(np.arange(64, dtype=f32) / f32(64)))).astype(f32)
    angc = (pos[None, :] * invc[:, None]).astype(f32)
    out["cos_c"] = np.cos(angc)[np.arange(128) % 64].astype(f32)
    out["sin_c"] = np.sin(angc)[np.arange(128) % 64].astype(f32)
    rot_c = np.zeros((128, 128), f32)
    for p in range(128):
        if p < 64:
            rot_c[p + 64, p] = -1.0
        else:
            rot_c[p - 64, p] = 1.0
    out["rot_c"] = rot_c.astype(NPBF)
    bones = np.zeros((128, 128), f32)
    bones[:64, :64] = 1.0
    bones[64:, 64:] = 1.0
    out["bones"] = bones
    out["ident"] = np.eye(128, dtype=f32).astype(NPBF)
    k = np.arange(128)[:, None]
    q = np.arange(128)[None, :]
    own = (k <= q).astype(f32)
    prevA = (k >= q).astype(f32)
    prevB = (k > q).astype(f32)
    hv = 1.0 if c > 0 else 0.0
    masks = np.zeros((128, 4, 256), f32)
    masks[:, 0] = np.concatenate([prevA, own], 1)
    masks[:, 1] = np.concatenate([prevA * hv, own], 1)
    masks[:, 2] = np.concatenate([prevB, own], 1)
    masks[:, 3] = np.concatenate([prevB * hv, own], 1)
    out["masks"] = masks.astype(NPBF)
    sel = np.zeros((65, 64), f32)
    sel[64, :] = 1.0
    out["sel65"] = sel
    lg = np.log1p(-(2.0 ** (-5.0 - np.arange(8, dtype=np.float64))))
    i = np.arange(128, dtype=np.float64)
    decT = np.zeros((128, 8, 128), np.float64)
    for h in range(8):
        rel = i[None, :] - i[:, None]
        decT[:, h, :] = np.where(rel >= 0, np.exp(lg[h] * np.maximum(rel, 0)), 0.0) / np.sqrt(128.0)
    out["decT"] = decT.astype(f32)
    qdec = np.exp(lg[None, :, None] * (i[None, None, :] + 1.0))
    out["qdec"] = np.ascontiguousarray(np.broadcast_to(qdec, (128, 8, 128))).astype(f32)
    kdec = np.exp(lg[None, :] * (127.0 - i[:, None])) / np.sqrt(128.0)
    out["kdec"] = kdec.astype(f32)
    coef = np.zeros((NCORE, 8), np.float64)
    for cp in range(c):
        coef[cp] = np.exp(lg * (TL * (c - 1 - cp)))
    out["ecoef"] = rep128(coef.reshape(-1))
    return out


def layer_weights(inputs, l):
    w_in = np.asarray(inputs["w_in"][l], np.float32)
    col = lambda i: w_in[:, OFF[i]:OFF[i + 1]]
    o = {}
    o[f"wq_a{l}"] = tile_w(col(0))
    o[f"wk_a{l}"] = tile_w(col(1))
    o[f"wv_a{l}"] = tile_wT(col(2))
    o[f"wq_b{l}"] = tile_w(col(3))
    o[f"wk_b{l}"] = tile_w(col(4))
    o[f"wv_b{l}"] = tile_wT(col(5))
    o[f"wq_c{l}"] = tile_w(col(6))
    o[f"wk_c{l}"] = tile_w(col(7))
    o[f"wv_c{l}"] = tile_wT(col(8))
    o[f"wg_c{l}"] = tile_w(col(9))
    o[f"wg_m{l}"] = tile_w(w_in[:, OFF[10]:OFF[13]])
    wbr = np.concatenate([inputs["w_br_a"][l], inputs["w_br_b"][l], inputs["w_br_c"][l]], 0).astype(np.float32)
    o[f"w_br{l}"] = tile_w(wbr)
    o[f"w_out{l}"] = tile_w(np.asarray(inputs["w_out"][l], np.float32))
    o[f"w_up{l}"] = tile_w(np.asarray(inputs["w_up"][l], np.float32))
    wd = np.asarray(inputs["w_down"][l], np.float32)
    o[f"w_dn{l}"] = np.stack([tile_w(wd[q * 2048:(q + 1) * 2048]) for q in range(4)], 0)
    o[f"g_mix{l}"] = np.ascontiguousarray(np.asarray(inputs["mix_norm"][l], np.float32).reshape(16, 128).T)
    o[f"g_mlp{l}"] = np.ascontiguousarray(np.asarray(inputs["mlp_norm"][l], np.float32).reshape(16, 128).T)
    for nm, key in (("gq_a", "a_q_norm"), ("gk_a", "a_k_norm"), ("gq_b", "b_q_norm"), ("gk_b", "b_k_norm")):
        v = np.asarray(inputs[key][l], np.float32)
        o[f"{nm}{l}"] = np.ascontiguousarray(np.tile(v, 2).reshape(128, 1))
    o[f"sinks{l}"] = rep128(inputs["b_sinks"][l])
    o[f"cgn{l}"] = np.ascontiguousarray(np.asarray(inputs["c_gn"][l], np.float32).reshape(8, 128).T)
    return o


def build_launch(stages, ext_in, ext_out):
    P = Prog(ext_in, ext_out)
    for fn in stages:
        fn(P)
    return P


def phase1_stages(l, xname):
    return [
        lambda P: stage_norm(P, xname, f"g_mix{l}", f"uT{l}"),
        lambda P: stage_qk_ab(P, l, "k"),
        lambda P: stage_c_rot(P, l, "k"),
        lambda P: stage_v(P, l),
        lambda P: stage_ret(P, l, True),
    ]


def phase2_stages(l, xname, x1name, x2name):
    return [
        lambda P: stage_qk_ab(P, l, "q"),
        lambda P: stage_c_rot(P, l, "q"),
        lambda P: stage_gates(P, l),
        lambda P: stage_attn(P, l),
        lambda P: stage_ret(P, l, False),
        lambda P: stage_merge(P, l),
        lambda P: stage_wout(P, l, xname, x1name),
        lambda P: stage_norm(P, x1name, f"g_mlp{l}", f"u2T{l}"),
        lambda P: stage_mlp(P, l, f"u2T{l}", x1name, x1name),
    ]


def run_launch(P, per_core_inputs, ncores):
    in_maps = []
    for c in range(ncores):
        m = {}
        for name, (shape, dt) in P.used_in.items():
            a = per_core_inputs[c][name]
            assert tuple(a.shape) == tuple(shape), (name, a.shape, shape)
            m[name] = a
        in_maps.append(m)
    res = run_bass_kernel_spmd(P.nc, in_maps, core_ids=list(range(ncores)))
    return res.results


def halo_from(prev, l):
    h = {}
    kA = prev[f"kT_A{l}"]
    VA = prev[f"V_A{l}"]
    khA = np.zeros_like(kA)
    vhA = np.zeros_like(VA)
    for g, d in enumerate(A_DIL):
        L = TL // d
        for r in range(d):
            khA[g * 384:(g + 1) * 384, r * 128:(r + 1) * 128] = kA[g * 384:(g + 1) * 384, (r + 1) * L - 128:(r + 1) * L]
            vhA[r * 128:(r + 1) * 128, g * 390:(g + 1) * 390] = VA[(r + 1) * L - 128:(r + 1) * L, g * 390:(g + 1) * 390]
    h[f"khalo_A{l}"] = khA
    h[f"vhalo_A{l}"] = vhA
    h[f"khalo_B{l}"] = np.ascontiguousarray(prev[f"kT_B{l}"][:, TL - 128:])
    h[f"vhalo_B{l}"] = np.ascontiguousarray(prev[f"V_B{l}"][TL - 128:, :])
    return h


P1_OUT = lambda l: [f"uT{l}", f"kT_A{l}", f"kT_B{l}", f"kT_C{l}", f"ktok_C{l}", f"V_A{l}", f"V_B{l}", f"V_C{l}", f"E{l}"]


def kernel(**inputs):
    x = np.asarray(inputs["x"], np.float32)[0]
    ncores = NCORE
    consts = [core_consts(c) for c in range(ncores)]
    xT = [np.ascontiguousarray(x[c * TL:(c + 1) * TL].T) for c in range(ncores)]
    cur = xT
    for l in range(2):
        lw = layer_weights(inputs, l)
        xname = f"xin{l}"
        P1 = build_launch(phase1_stages(l, xname), ext_in=set(lw) | set(consts[0]) | {xname}, ext_out=set(P1_OUT(l)))
        pci = [dict(lw, **consts[c], **{xname: cur[c]}) for c in range(ncores)]
        r1 = run_launch(P1, pci, ncores)
        Eall = np.stack([r1[c][f"E{l}"] for c in range(ncores)], 0)
        zero_prev = {k: np.zeros_like(v) for k, v in r1[0].items()}
        pci2 = []
        for c in range(ncores):
            d = dict(lw, **consts[c])
            d.update({k: r1[c][k] for k in P1_OUT(l)})
            d.update(halo_from(r1[c - 1] if c > 0 else zero_prev, l))
            d[f"Eall{l}"] = Eall
            d[xname] = cur[c]
            pci2.append(d)
        x1 = f"xmid{l}"
        P2 = build_launch(phase2_stages(l, xname, x1, x1), ext_in=set(pci2[0]), ext_out={x1})
        r2 = run_launch(P2, pci2, ncores)
        cur = [r2[c][x1] for c in range(ncores)]
    out = np.concatenate([cur[c].T for c in range(ncores)], 0)[None]
    return np.ascontiguousarray(out.astype(np.float32))
```

```python
import numpy as np
import ml_dtypes
import concourse.bass as bass
import concourse.mybir as mybir
from concourse.bass_utils import run_bass_kernel_spmd

F32 = mybir.dt.float32
BF16 = mybir.dt.bfloat16
AF = mybir.ActivationFunctionType
ALU = mybir.AluOpType
NPBF = ml_dtypes.bfloat16

TL = 2048
NCORE = 8
SEQ = 16384
EPS = 1e-6
SAME_ENGINE_SYNC = True
A_DIL = (1, 4, 16)

_uid = [0]


def uid():
    _uid[0] += 1
    return _uid[0]


class Stage:
    ENG = ("pe", "act", "dve", "pool", "sp")

    def __init__(self, nc):
        self.nc = nc
        self.ops = {e: [] for e in self.ENG}
        self.cnt = {e: 0 for e in self.ENG}
        self.dcnt = {}
        self.last_w = {}
        self.readers = {}
        self.seen = {e: {} for e in self.ENG}
        self.nsb = 0

    def sb(self, shape, dt, name="t"):
        return self.nc.alloc_sbuf_tensor(f"{name}_{uid()}", list(shape), dt)

    def ps(self, shape, dt=F32, name="p"):
        return self.nc.alloc_psum_tensor(f"{name}_{uid()}", list(shape), dt)

    PSUM_NAMES = ("psm", "ss", "aux_ss", "aux_rot", "psT", "psT_", "oacc", "sps", "bc", "ps_s", "ps_y", "pkv", "ps")

    def op(self, eng, fn, reads=(), writes=(), signal=True, dma=None):
        def is_ps(k):
            nm = k[0] if isinstance(k, tuple) else k
            return nm in self.PSUM_NAMES
        if eng != "pe":
            pr = [k for k in reads if is_ps(k)]
            if pr:
                reads = [k for k in reads if not is_ps(k)]
                writes = list(writes) + [k for k in pr if k not in writes]
        waits = {}

        def need(tok):
            if tok is None:
                return
            k, v = tok
            if k == eng and (eng in ("pe", "sp") or not SAME_ENGINE_SYNC):
                return
            if waits.get(k, 0) < v:
                waits[k] = v

        for b in reads:
            need(self.last_w.get(b))
        for b in writes:
            need(self.last_w.get(b))
            for t in self.readers.get(b, ()):
                need(t)
        wl = []
        for k, v in waits.items():
            if self.seen[eng].get(k, 0) >= v:
                continue
            self.seen[eng][k] = v
            wl.append((k, v))
        if dma is not None:
            key = ("dma", dma)
            self.dcnt[key] = self.dcnt.get(key, 0) + 16
            tok = (key, self.dcnt[key])
            inc = tok
        else:
            if signal:
                self.cnt[eng] += 1
                tok = (eng, self.cnt[eng])
                inc = tok
            else:
                tok = (eng, self.cnt[eng] + 1)
                inc = None
        self.ops[eng].append((wl, fn, inc))
        for b in writes:
            self.last_w[b] = tok
            self.readers[b] = []
        for b in reads:
            self.readers.setdefault(b, []).append(tok)

    def dma(self, q, out, in_, reads, writes, key):
        self.op(q, lambda e: e.dma_start(out=out, in_=in_), reads, writes, dma=key)

    def act(self, out, in_, func, reads, writes, **kw):
        self.op("act", lambda e: e.activation(out=out, in_=in_, func=func, **kw), reads, writes)

    def mm(self, out, lhsT, rhs, start, stop, reads, writes, signal=True):
        self.op("pe", lambda e: e.matmul(out, lhsT=lhsT, rhs=rhs, start=start, stop=stop), reads, writes,
                signal=signal)

    def tr(self, out, in_, ident, reads, writes):
        self.op("pe", lambda e: e.transpose(out, in_, ident), reads, writes)

    def tt(self, eng, out, in0, in1, op, reads, writes):
        self.op(eng, lambda e: e.tensor_tensor(out=out, in0=in0, in1=in1, op=op), reads, writes)

    def stt(self, eng, out, in0, scalar, in1, op0, op1, reads, writes):
        self.op(eng, lambda e: e.scalar_tensor_tensor(out=out, in0=in0, scalar=scalar, in1=in1, op0=op0, op1=op1),
                reads, writes)

    def ts(self, eng, out, in0, s1, op0, reads, writes, s2=None, op1=None):
        if op1 is None:
            self.op(eng, lambda e: e.tensor_scalar(out=out, in0=in0, scalar1=s1, scalar2=None, op0=op0), reads, writes)
        else:
            self.op(eng, lambda e: e.tensor_scalar(out=out, in0=in0, scalar1=s1, scalar2=s2, op0=op0, op1=op1),
                    reads, writes)

    def copy(self, eng, out, in_, reads, writes):
        self.op(eng, lambda e: e.tensor_copy(out=out, in_=in_), reads, writes)

    def recip(self, out, in_, reads, writes):
        self.op("dve", lambda e: e.reciprocal(out=out, in_=in_), reads, writes)

    def memset(self, eng, ap, val, writes):
        self.op(eng, lambda e: e.memset(ap, val), (), writes)

    def emit(self):
        nc = self.nc
        finals = [(k, v) for k, v in self.dcnt.items()]
        sems = {}
        for e in self.ENG:
            if self.cnt[e] > 0:
                sems[e] = nc.alloc_semaphore(f"s{e}{uid()}")
        for k in self.dcnt:
            sems[k] = nc.alloc_semaphore(f"sd{uid()}")
        ops = self.ops
        with nc.Block() as block:
            def mk(eng):
                def body(e):
                    for wl, fn, inc in ops[eng]:
                        for k, v in wl:
                            e.wait_ge(sems[k], v)
                        ins = fn(e)
                        if inc is not None:
                            ins.then_inc(sems[inc[0]], 16 if isinstance(inc[0], tuple) else 1)
                    if eng == "sp":
                        for k, v in finals:
                            e.wait_ge(sems[k], v)
                return body

            block.tensor(mk("pe"))
            block.scalar(mk("act"))
            block.vector(mk("dve"))
            block.gpsimd(mk("pool"))
            block.sync(mk("sp"))


class Prog:
    def __init__(self, ext_in, ext_out):
        self.nc = bass.Bass("TRN2", target_bir_lowering=False)
        self.ext_in = set(ext_in)
        self.ext_out = set(ext_out)
        self.t = {}
        self.used_in = {}
        self.used_out = {}

    def dram(self, name, shape, dt):
        if name in self.t:
            return self.t[name]
        if name in self.ext_in:
            kind = "ExternalInput"
            self.used_in[name] = (tuple(shape), dt)
        elif name in self.ext_out:
            kind = "ExternalOutput"
            self.used_out[name] = (tuple(shape), dt)
        else:
            kind = "Internal"
        ap = self.nc.dram_tensor(name, list(shape), dt, kind=kind).ap()
        self.t[name] = ap
        return ap


def slots(st, n, shape, dt, name):
    return [st.sb(shape, dt, name) for _ in range(n)]


def stage_norm(P, xname, gname, uname):
    nc = P.nc
    xT = P.dram(xname, [2048, TL], F32)
    g = P.dram(gname, [128, 16], F32)
    uT = P.dram(uname, [2048, TL], BF16)
    with nc.cleanup_on_exit():
        st = Stage(nc)
        xs = st.sb([128, 16, 512], F32, "xs")
        sq = slots(st, 3, [128, 512], F32, "sq")
        gt = st.sb([128, 16], F32, "gt")
        ones = st.sb([128, 128], F32, "ones")
        epst = st.sb([128, 1], F32, "eps")
        std = slots(st, 2, [128, 512], F32, "std")
        rstd = slots(st, 2, [128, 512], F32, "rstd")
        ub = slots(st, 4, [128, 512], BF16, "ub")
        ss = [st.ps([128, 512]) for _ in range(2)]
        st.dma("sp", gt[:], g, [], ["gt"], "gt")
        st.memset("pool", ones[:], 1.0, ["ones"])
        st.memset("pool", epst[:], EPS, ["eps"])
        k = 0
        ku = 0
        for tb in range(TL // 512):
            cs = slice(tb * 512, (tb + 1) * 512)
            for c in range(16):
                st.dma("sp", xs[:, c, :], xT[c * 128:(c + 1) * 128, cs], [], [("xs", c)], ("xs", c))
            for c in range(16):
                st.act(sq[k % 3][:], xs[:, c, :], AF.Square, [("xs", c)], [("sq", k % 3)])
                st.mm(ss[tb % 2][:], ones[:], sq[k % 3][:], c == 0, c == 15, ["ones", ("sq", k % 3)],
                      [("ss", tb % 2)])
                k += 1
            s2 = tb % 2
            st.act(std[s2][:], ss[s2][:], AF.Sqrt, [("ss", s2)], [("std", s2)], bias=EPS, scale=1.0 / 2048)
            st.recip(rstd[s2][:], std[s2][:], [("std", s2)], [("rstd", s2)])
            for c in range(16):
                u = ku % 4
                ku += 1
                st.stt("dve", ub[u][:], xs[:, c, :], gt[:, c:c + 1], rstd[s2][:], ALU.mult, ALU.mult,
                       [("xs", c), "gt", ("rstd", s2)], [("ub", u)])
                st.dma("sp", uT[c * 128:(c + 1) * 128, cs], ub[u][:], [("ub", u)], [], ("ub", u))
        st.emit()
        nc.all_engine_barrier()


def load_inT(st, in_ap, KC, T=TL, t0=0, name="in"):
    sbt = st.sb([128, KC, T], BF16, name)
    keys = []
    for kc in range(KC):
        key = (name, kc)
        st.dma("sp", sbt[:, kc, :], in_ap[kc * 128:(kc + 1) * 128, t0:t0 + T], [], [key], key)
        keys.append(key)
    return sbt, keys


def linF(st, in_sb, in_keys, KC, jobs, T=TL, nbank=4, conv_eng=("pool",)):
    nc = st.nc
    wst = slots(st, 3, [128, KC * 128], F32, "wst")
    wbf = slots(st, 2, [128, KC * 128], BF16, "wbf")
    ps = [st.ps([128, 512]) for _ in range(nbank)]
    pfx = uid()
    seq = [(j, n) for j, job in enumerate(jobs) for n in range(job[1])]

    def load(i):
        j, n = seq[i]
        s = i % 3
        st.dma("sp", wst[s][:], jobs[j][0][n], [], [("wst", pfx, s)], ("wst", pfx, s))

    def conv(i):
        s3 = i % 3
        s2 = i % 2
        ce = conv_eng[i % len(conv_eng)]
        if ce == "act":
            st.act(wbf[s2][:], wst[s3][:], AF.Copy, [("wst", pfx, s3)], [("wbf", pfx, s2)])
        else:
            st.copy(ce, wbf[s2][:], wst[s3][:], [("wst", pfx, s3)], [("wbf", pfx, s2)])

    pending = []

    def run_pending(flush):
        while True:
            todo = pending[:-1] if not flush else pending
            if not flush:
                for phl in reversed(todo):
                    phl.pop(0)()
                pending[:] = [p for p in pending if p]
                return
            if not pending:
                return
            for phl in reversed(list(pending)):
                phl.pop(0)()
            pending[:] = [p for p in pending if p]

    for i in range(min(3, len(seq))):
        load(i)
    conv(0)
    u = 0
    for i, (j, n) in enumerate(seq):
        if i + 3 < len(seq):
            load(i + 3)
        if i + 1 < len(seq):
            conv(i + 1)
        s2 = i % 2
        for tb in range(T // 512):
            b = u % nbank
            u += 1
            for kc in range(KC):
                st.mm(ps[b][:], wbf[s2][:, kc * 128:(kc + 1) * 128], in_sb[:, kc, tb * 512:(tb + 1) * 512],
                      kc == 0, kc == KC - 1, [("wbf", pfx, s2), in_keys[kc]], [("psm", pfx, b)],
                      signal=(kc == KC - 1))
            ph = jobs[j][2](n, tb, ps[b], ("psm", pfx, b))
            if ph:
                pending.append(list(ph))
            run_pending(False)
    run_pending(True)


def h_act(st, func, out_dram, row0=0, store_q="sp"):
    stg = slots(st, 2, [128, TL], BF16, "stg")
    pfx = uid()

    def h(n, tb, ps, pskey):
        sk = n % 2
        st.act(stg[sk][:, tb * 512:(tb + 1) * 512], ps[:], func, [pskey], [("stg", pfx, sk, tb)])
        if tb == TL // 512 - 1:
            st.dma(store_q, out_dram[row0 + n * 128: row0 + (n + 1) * 128, :], stg[sk][:],
                   [("stg", pfx, sk, t) for t in range(TL // 512)], [], ("stg", pfx, sk))

    return h


def h_relu2_sb(st, dst_sb, dst_keys):
    r = slots(st, 3, [128, 512], F32, "r")
    pfx = uid()
    c = [0]

    def h(n, tb, ps, pskey):
        k = c[0] % 3
        c[0] += 1
        st.act(r[k][:], ps[:], AF.Relu, [pskey], [("r", pfx, k)])
        st.tt("pool", dst_sb[:, n, tb * 512:(tb + 1) * 512], r[k][:], r[k][:], ALU.mult, [("r", pfx, k)],
              [(dst_keys, n, tb)])

    return h


def h_resid(st, x_in, x_out, xkey):
    if not hasattr(st, "_xo"):
        st._xo = slots(st, 2, [128, TL], F32, "xo")
        st._xn = slots(st, 2, [128, TL], F32, "xn")
        st._xc = [0]
    xo, xn, c = st._xo, st._xn, st._xc
    pfx = "rs"

    def h(n, tb, ps, pskey):
        if tb == 0:
            c[0] += 1
        k = c[0] % 2
        rs = slice(n * 128, (n + 1) * 128)
        cs = slice(tb * 512, (tb + 1) * 512)
        if tb == 0:
            st.dma("sp", xo[k][:], x_in[rs, :], [(xkey, n)], [("xo", pfx, k)], ("xo", pfx, k))
        st.tt("dve", xn[k][:, cs], ps[:], xo[k][:, cs], ALU.add, [pskey, ("xo", pfx, k)], [("xn", pfx, k, tb)])
        if tb == TL // 512 - 1:
            st.dma("sp", x_out[rs, :], xn[k][:], [("xn", pfx, k, t) for t in range(TL // 512)], [(xkey, n)],
                   ("xn", pfx, k))

    return h


class RopeCtx:
    def __init__(self, st, P, kind, core_tabs=True):
        nc = st.nc
        self.st = st
        self.kind = kind
        sfx = "ab" if kind == "ab" else "c"
        cos_d = P.dram("cos_" + sfx, [128, TL], F32)
        sin_d = P.dram("sin_" + sfx, [128, TL], F32)
        rot_d = P.dram("rot_" + sfx, [128, 128], BF16)
        self.C = st.sb([128, TL], F32, "cos")
        self.S = st.sb([128, TL], F32, "sin")
        self.R = st.sb([128, 128], BF16, "rot")
        st.dma("sp", self.C[:], cos_d, [], ["cos"], "cos")
        st.dma("sp", self.S[:], sin_d, [], ["sin"], "sin")
        st.dma("sp", self.R[:], rot_d, [], ["rot"], "rot")
        self.aux_rot = [st.ps([128, 512]) for _ in range(2)]
        if kind == "ab":
            bo_d = P.dram("bones", [128, 128], F32)
            self.bones = st.sb([128, 128], F32, "bones")
            st.dma("sp", self.bones[:], bo_d, [], ["bones"], "bones")
            self.aux_ss = [st.ps([128, 512]) for _ in range(2)]
            self.epst = st.sb([128, 1], F32, "eps")
            st.memset("pool", self.epst[:], EPS, ["eps"])
            self.sq = slots(st, 2, [128, 512], F32, "sq")
            self.std = slots(st, 2, [128, 512], F32, "std")
            self.rstd = slots(st, 2, [128, 512], F32, "rstd")
        self.qn = slots(st, 2, [128, 512], BF16, "qn")
        self.t1 = slots(st, 2, [128, 512], F32, "t1")
        self.t2 = slots(st, 2, [128, 512], F32, "t2")
        self.cnt = 0


def h_qk(ctx, gain_sb, gain_key, dil_of_chunk, out_dram, row0=0):
    st = ctx.st
    stg = slots(st, 2, [128, TL], BF16, "stgq")
    pfx = uid()

    def h(n, tb, ps, pskey):
        k = ctx.cnt % 2
        ctx.cnt += 1
        cs = slice(tb * 512, (tb + 1) * 512)

        def p0():
            st.act(ctx.sq[k][:], ps[:], AF.Square, [pskey], [("sq", k)])
            st.mm(ctx.aux_ss[k][:], ctx.bones[:], ctx.sq[k][:], True, True, ["bones", ("sq", k)], [("aux_ss", k)])

        def p1():
            st.act(ctx.std[k][:], ctx.aux_ss[k][:], AF.Sqrt, [("aux_ss", k)], [("std", k)],
                   bias=EPS, scale=1.0 / 64)
            st.recip(ctx.rstd[k][:], ctx.std[k][:], [("std", k)], [("rstd", k)])
            st.stt("dve", ctx.qn[k][:], ps[:], gain_sb, ctx.rstd[k][:], ALU.mult, ALU.mult,
                   [pskey, gain_key, ("rstd", k)], [("qn", k)])
            st.mm(ctx.aux_rot[k][:], ctx.R[:], ctx.qn[k][:], True, True, ["rot", ("qn", k)], [("aux_rot", k)])

        def p2():
            st.tt("pool", ctx.t1[k][:], ctx.qn[k][:], ctx.C[:, cs], ALU.mult, [("qn", k), "cos"], [("t1", k)])
            st.tt("dve", ctx.t2[k][:], ctx.aux_rot[k][:], ctx.S[:, cs], ALU.mult, [("aux_rot", k), "sin"],
                  [("t2", k)])
            d = dil_of_chunk(n)
            sk = n % 2
            w = 512 // d
            o_ap = stg[sk][:].rearrange("p (r j) -> p r j", r=d)[:, :, tb * w:(tb + 1) * w]
            a1 = ctx.t1[k][:].rearrange("p (j r) -> p r j", r=d)
            a2 = ctx.t2[k][:].rearrange("p (j r) -> p r j", r=d)
            st.tt("pool", o_ap, a1, a2, ALU.add, [("t1", k), ("t2", k)], [("stgq", pfx, sk, tb)])
            if tb == TL // 512 - 1:
                st.dma("sp", out_dram[row0 + n * 128: row0 + (n + 1) * 128, :], stg[sk][:],
                       [("stgq", pfx, sk, t) for t in range(TL // 512)], [], ("stgq", pfx, sk))

        return [p0, p1, p2]

    return h


def h_crot(ctx, out_dram, tok_out=None, kdec_sb=None, ident=None):
    st = ctx.st
    stg = slots(st, 2, [128, TL], BF16, "stgc")
    pfx = uid()
    if tok_out is not None:
        psT = [st.ps([128, 512], F32, "psT") for _ in range(2)]
        kt = slots(st, 2, [128, 16, 128], BF16, "kt")

    def h(n, tb, ps, pskey):
        k = ctx.cnt % 2
        ctx.cnt += 1
        cs = slice(tb * 512, (tb + 1) * 512)

        def p0():
            st.act(ctx.qn[k][:], ps[:], AF.Copy, [pskey], [("qn", k)])
            st.mm(ctx.aux_rot[k][:], ctx.R[:], ctx.qn[k][:], True, True, ["rot", ("qn", k)], [("aux_rot", k)])

        def p1():
            h_tail(n, tb, ps, pskey, k, cs)

        return [p0, p1]

    def h_tail(n, tb, ps, pskey, k, cs):
        st.tt("dve", ctx.t1[k][:], ps[:], ctx.C[:, cs], ALU.mult, [pskey, "cos"], [("t1", k)])
        st.tt("dve", ctx.t2[k][:], ctx.aux_rot[k][:], ctx.S[:, cs], ALU.mult, [("aux_rot", k), "sin"], [("t2", k)])
        sk = n % 2
        st.tt("pool", stg[sk][:, cs], ctx.t1[k][:], ctx.t2[k][:], ALU.add, [("t1", k), ("t2", k)],
              [("stgc", pfx, sk, tb)])
        if tb == TL // 512 - 1:
            allk = [("stgc", pfx, sk, t) for t in range(TL // 512)]
            st.dma("sp", out_dram[n * 128:(n + 1) * 128, :], stg[sk][:], allk, [], ("stgc", pfx, sk))
            if tok_out is not None:
                for blk in range(16):
                    pk = blk % 2
                    st.mm(psT[pk][:, 0:128], stg[sk][:, blk * 128:(blk + 1) * 128], ident[:], True, True,
                          allk + ["ident"], [("psT", pk)])
                    st.ts("dve", kt[sk][:, blk, :], psT[pk][:, 0:128], kdec_sb[:, n:n + 1], ALU.mult,
                          [("psT", pk), "kdec"], [("kt", pfx, sk, blk)])
                st.dma("sp", tok_out[n].rearrange("(b p) d -> p b d", p=128), kt[sk][:],
                       [("kt", pfx, sk, b) for b in range(16)], [], ("kt", pfx, sk))

    return h


def linT(st, in_sb, in_keys, wT_ap, c0, ncols, d, out_dram, col0, heads, ps_list, vst, wst, wv, pfx):
    for kc in range(16):
        s = kc % len(wst)
        st.dma("sp", wst[s][:, 0:ncols], wT_ap[:, kc, c0:c0 + ncols], [], [("wstT", s)], ("wstT", s))
        st.copy("pool", wv[:, kc, 0:ncols], wst[s][:, 0:ncols], [("wstT", s)], [("wv", kc)])
    nb = 16 // d
    for pb in range(16):
        r, b = pb // nb, pb % nb
        tsl = slice(b * 128 * d + r, b * 128 * d + r + 127 * d + 1, d)
        pbank = pb % len(ps_list)
        ps = ps_list[pbank]
        for kc in range(16):
            st.mm(ps[:, 0:ncols], in_sb[:, kc, tsl], wv[:, kc, 0:ncols], kc == 0, kc == 15,
                  [in_keys[kc], ("wv", kc)], [("psT_", pbank)], signal=(kc == 15))
        vs = pb % len(vst)
        if heads:
            st.act(vst[vs][:, 0:heads, 0:64], ps[:, 0:ncols].rearrange("p (h e) -> p h e", e=64), AF.Copy,
                   [("psT_", pbank)], [("vst", vs)])
            st.dma("sp", out_dram[pb * 128:(pb + 1) * 128, col0:col0 + heads * 65],
                   vst[vs][:, 0:heads, :].rearrange("p h e -> p (h e)"), [("vst", vs)], [], ("vst", vs))
        else:
            st.act(vst[vs][:, 0:ncols], ps[:, 0:ncols], AF.Copy, [("psT_", pbank)], [("vst", vs)])
            st.dma("sp", out_dram[pb * 128:(pb + 1) * 128, col0:col0 + ncols], vst[vs][:, 0:ncols],
                   [("vst", vs)], [], ("vst", vs))


def stage_qk_ab(P, l, which):
    nc = P.nc
    uT = P.dram(f"uT{l}", [2048, TL], BF16)
    na, nb_ = 9, (2 if which == "k" else 8)
    wa = P.dram(f"w{which}_a{l}", [na, 128, 2048], F32)
    wb = P.dram(f"w{which}_b{l}", [nb_, 128, 2048], F32)
    ga = P.dram(f"g{which}_a{l}", [128, 1], F32)
    gb = P.dram(f"g{which}_b{l}", [128, 1], F32)
    oa = P.dram(f"{which}T_A{l}", [1152, TL], BF16)
    ob = P.dram(f"{which}T_B{l}", [nb_ * 128, TL], BF16)
    with nc.cleanup_on_exit():
        st = Stage(nc)
        in_sb, in_keys = load_inT(st, uT, 16)
        ctx = RopeCtx(st, P, "ab")
        gat = st.sb([128, 1], F32, "ga")
        gbt = st.sb([128, 1], F32, "gb")
        st.dma("sp", gat[:], ga, [], ["ga"], "ga")
        st.dma("sp", gbt[:], gb, [], ["gb"], "gb")
        jobs = [
            (wa, na, h_qk(ctx, gat[:], "ga", lambda n: A_DIL[(2 * n) // 6], oa)),
            (wb, nb_, h_qk(ctx, gbt[:], "gb", lambda n: 1, ob)),
        ]
        linF(st, in_sb, in_keys, 16, jobs)
        st.emit()
        nc.all_engine_barrier()


def stage_c_rot(P, l, which):
    nc = P.nc
    uT = P.dram(f"uT{l}", [2048, TL], BF16)
    w = P.dram(f"w{which}_c{l}", [8, 128, 2048], F32)
    o = P.dram(f"{which}T_C{l}", [1024, TL], BF16)
    with nc.cleanup_on_exit():
        st = Stage(nc)
        in_sb, in_keys = load_inT(st, uT, 16)
        ctx = RopeCtx(st, P, "c")
        if which == "k":
            tok = P.dram(f"ktok_C{l}", [8, TL, 128], BF16)
            kdec_d = P.dram("kdec", [128, 8], F32)
            id_d = P.dram("ident", [128, 128], BF16)
            kdec = st.sb([128, 8], F32, "kdec")
            ident = st.sb([128, 128], BF16, "ident")
            st.dma("sp", kdec[:], kdec_d, [], ["kdec"], "kdec")
            st.dma("sp", ident[:], id_d, [], ["ident"], "ident")
            h = h_crot(ctx, o, tok, kdec, ident)
        else:
            h = h_crot(ctx, o)
        linF(st, in_sb, in_keys, 16, [(w, 8, h)])
        st.emit()
        nc.all_engine_barrier()


def stage_v(P, l):
    nc = P.nc
    uT = P.dram(f"uT{l}", [2048, TL], BF16)
    wva = P.dram(f"wv_a{l}", [128, 16, 1152], F32)
    wvb = P.dram(f"wv_b{l}", [128, 16, 256], F32)
    wvc = P.dram(f"wv_c{l}", [128, 16, 1024], F32)
    VA = P.dram(f"V_A{l}", [TL, 18 * 65], BF16)
    VB = P.dram(f"V_B{l}", [TL, 4 * 65], BF16)
    VC = P.dram(f"V_C{l}", [TL, 1024], BF16)
    with nc.cleanup_on_exit():
        st = Stage(nc)
        in_sb, in_keys = load_inT(st, uT, 16)
        ps_list = [st.ps([128, 512]) for _ in range(4)]
        vst = slots(st, 3, [128, 6, 65], BF16, "vst")
        vstc = slots(st, 3, [128, 512], BF16, "vstc")
        wst = slots(st, 3, [128, 512], F32, "wstT")
        wv = st.sb([128, 16, 512], BF16, "wv")
        for i in range(3):
            st.memset("pool", vst[i][:], 1.0, [("vst", i)])
        pfx = uid()
        for g in range(3):
            linT(st, in_sb, in_keys, wva, g * 384, 384, A_DIL[g], VA, g * 390, 6, ps_list, vst, wst, wv, pfx)
        linT(st, in_sb, in_keys, wvb, 0, 256, 1, VB, 0, 4, ps_list, vst, wst, wv, pfx)
        for half in range(2):
            linT(st, in_sb, in_keys, wvc, half * 512, 512, 1, VC, half * 512, 0, ps_list, vstc, wst, wv, pfx)
        st.emit()
        nc.all_engine_barrier()


def stage_gates(P, l):
    nc = P.nc
    uT = P.dram(f"uT{l}", [2048, TL], BF16)
    wcg = P.dram(f"wg_c{l}", [8, 128, 2048], F32)
    wg = P.dram(f"wg_m{l}", [48, 128, 2048], F32)
    scg = P.dram(f"scgT{l}", [1024, TL], BF16)
    sg = P.dram(f"sgT{l}", [6144, TL], BF16)
    with nc.cleanup_on_exit():
        st = Stage(nc)
        in_sb, in_keys = load_inT(st, uT, 16)
        jobs = [(wg, 48, h_act(st, AF.Sigmoid, sg)), (wcg, 8, h_act(st, AF.Silu, scg))]
        linF(st, in_sb, in_keys, 16, jobs)
        st.emit()
        nc.all_engine_barrier()


def stage_attn(P, l):
    nc = P.nc
    qA = P.dram(f"qT_A{l}", [1152, TL], BF16)
    kA = P.dram(f"kT_A{l}", [1152, TL], BF16)
    khA = P.dram(f"khalo_A{l}", [1152, TL], BF16)
    VA = P.dram(f"V_A{l}", [TL, 1170], BF16)
    VhA = P.dram(f"vhalo_A{l}", [TL, 1170], BF16)
    qB = P.dram(f"qT_B{l}", [1024, TL], BF16)
    kB = P.dram(f"kT_B{l}", [256, TL], BF16)
    khB = P.dram(f"khalo_B{l}", [256, 128], BF16)
    VB = P.dram(f"V_B{l}", [TL, 260], BF16)
    VhB = P.dram(f"vhalo_B{l}", [128, 260], BF16)
    masks_d = P.dram("masks", [128, 4, 256], BF16)
    sink_d = P.dram(f"sinks{l}", [128, 16], F32)
    sel_d = P.dram("sel65", [65, 64], F32)
    oA = P.dram(f"oT_A{l}", [384, TL], BF16)
    oB = P.dram(f"oT_B{l}", [1024, TL], BF16)
    with nc.cleanup_on_exit():
        st = Stage(nc)
        Vsb = st.sb([128, 16, 1170], BF16, "Vsb")
        Vh = st.sb([128, 16, 1170], BF16, "Vh")
        VBs = st.sb([128, 16, 260], BF16, "VBs")
        VBh = st.sb([128, 260], BF16, "VBh")
        masks = st.sb([128, 4, 256], BF16, "masks")
        sink = st.sb([128, 16], F32, "sink")
        esink = st.sb([128, 16], F32, "esink")
        sel = st.sb([65, 64], F32, "sel")
        for b4 in range(4):
            rr = slice(b4 * 512, (b4 + 1) * 512)
            bb = slice(b4 * 4, (b4 + 1) * 4)
            st.dma("sp", Vsb[:, bb, :], VA[rr, :].rearrange("(b p) e -> p b e", p=128), [], [("Vsb", b4)], ("Vsb", b4))
            st.dma("sp", Vh[:, bb, :], VhA[rr, :].rearrange("(b p) e -> p b e", p=128), [], [("Vh", b4)], ("Vh", b4))
            st.dma("sp", VBs[:, bb, :], VB[rr, :].rearrange("(b p) e -> p b e", p=128), [], [("VBs", b4)], ("VBs", b4))
        st.dma("sp", VBh[:], VhB, [], ["VBh"], "VBh")
        st.dma("sp", masks[:], masks_d, [], ["masks"], "masks")
        st.dma("sp", sink[:], sink_d, [], ["sink"], "sink")
        st.dma("sp", sel[:], sel_d, [], ["sel"], "sel")
        st.act(esink[:], sink[:], AF.Exp, ["sink"], ["esink"])
        NQ = 3
        qs = slots(st, NQ, [64, TL], BF16, "qs")
        ks = slots(st, NQ, [64, TL], BF16, "ks")
        khs = slots(st, NQ, [64, TL], BF16, "khs")
        oacc = [st.ps([128, 512]) for _ in range(4)]
        sps_t = [st.ps([128, 512]) for _ in range(3)]
        sps = [sps_t[i][:, 0:256] for i in range(3)]
        bc = st.ps([128, 512])
        ex = slots(st, 4, [128, 256], BF16, "ex")
        pm = slots(st, 4, [128, 256], BF16, "pm")
        osb = slots(st, 2, [65, 512], F32, "osb")
        den = slots(st, 2, [64, 512], F32, "den")
        ostg = slots(st, 2, [64, TL], BF16, "ostg")

        units = []
        state = {"hq": 0, "slot_started": None}

        def add_head(q_rows, k_rows, kh_rows, kh_cols, d, Vt, Vht, vcol0, mask_n, mask_f, is_B, hs):
            s = hs % NQ
            st.dma("sp", qs[s][:], q_rows, [], [("qs", s)], ("qs", s))
            st.dma("sp", ks[s][:], k_rows, [], [("ks", s)], ("ks", s))
            st.dma("sp", khs[s][:, 0:kh_cols], kh_rows, [], [("khs", s)], ("khs", s))
            nb = 16 // d
            for r in range(d):
                for b in range(nb):
                    pb = r * nb + b
                    c0 = pb * 128
                    uu = dict(q=qs[s][:, c0:c0 + 128], kown=ks[s][:, c0:c0 + 128], s=s, pb=pb, d=d, r=r, b=b)
                    if b > 0:
                        uu["kprev"] = ks[s][:, c0 - 128:c0]
                        uu["kpk"] = ("ks", s)
                        uu["vprev"] = Vt[:, pb - 1, vcol0:vcol0 + 65]
                        uu["vpk"] = (("VBs" if is_B else "Vsb"), (pb - 1) // 4)
                        uu["mask"] = mask_n
                    else:
                        uu["kprev"] = khs[s][:, r * 128:(r + 1) * 128]
                        uu["kpk"] = ("khs", s)
                        if is_B:
                            uu["vprev"] = Vht[:, vcol0:vcol0 + 65]
                            uu["vpk"] = "VBh"
                        else:
                            uu["vprev"] = Vht[:, r, vcol0:vcol0 + 65]
                            uu["vpk"] = ("Vh", r // 4)
                        uu["mask"] = mask_f
                    uu["vown"] = Vt[:, pb, vcol0:vcol0 + 65]
                    uu["vok"] = (("VBs" if is_B else "Vsb"), pb // 4)
                    outs = []
                    if d == 1:
                        outs.append((b // 4, slice((b % 4) * 128, (b % 4) * 128 + 128), slice(0, 128)))
                    elif d == 4:
                        outs.append((b, slice(r, 512, 4), slice(0, 128)))
                    else:
                        for k4 in range(4):
                            outs.append((k4, slice(r, 512, 16), slice(k4 * 32, k4 * 32 + 32)))
                    uu["outs"] = outs
                    units.append(uu)

        def emit_S(i, uu):
            sl = i % 4
            s2_ = i % 3
            st.mm(sps[s2_][:, 0:128], uu["kprev"], uu["q"], True, True, [uu["kpk"], ("qs", uu["s"])],
                  [("sps", s2_)])
            st.mm(sps[s2_][:, 128:256], uu["kown"], uu["q"], True, True, [("ks", uu["s"]), ("qs", uu["s"])],
                  [("sps", s2_)])
            st.act(ex[sl][:], sps[s2_], AF.Exp, [("sps", s2_)], [("ex", sl)], scale=0.125)
            st.tt("pool", pm[sl][:], ex[sl][:], masks[:, uu["mask"], :], ALU.mult, [("ex", sl), "masks"],
                  [("pm", sl)])

        def emit_PV(i, uu, started):
            sl = i % 4
            for half, (vap, vk) in enumerate(((uu["vprev"], uu["vpk"]), (uu["vown"], uu["vok"]))):
                for (bank, osl, rsl) in uu["outs"]:
                    first = not started[bank]
                    started[bank] = True
                    rhs = pm[sl][:, half * 128 + rsl.start: half * 128 + rsl.stop]
                    st.mm(oacc[bank][0:65, osl], vap, rhs, first, False, [vk, ("pm", sl)], [("oacc", bank)])

        def run_units(started):
            LAG = 2
            n = len(units)
            for i in range(n + LAG):
                if i < n:
                    emit_S(state["hq"] + i, units[i])
                if i >= LAG:
                    emit_PV(state["hq"] + i - LAG, units[i - LAG], started)
            state["hq"] += n
            units.clear()

        def normalize(out_rows, sink_col, oslot):
            for bank in range(4):
                k = bank % 2
                st.act(osb[k][:], oacc[bank][0:65, :], AF.Copy, [("oacc", bank)], [("osb", k)])
                st.mm(bc[0:64, :], sel[:], osb[k][:], True, True, ["sel", ("osb", k)], ["bc"])
                if sink_col is not None:
                    st.ts("dve", den[k][:], bc[0:64, :], esink[0:64, sink_col:sink_col + 1], ALU.add,
                          ["bc", "esink"], [("den", k)])
                    st.recip(den[k][:], den[k][:], [("den", k)], [("den", k)])
                else:
                    st.recip(den[k][:], bc[0:64, :], ["bc"], [("den", k)])
                st.tt("dve", ostg[oslot][:, bank * 512:(bank + 1) * 512], osb[k][0:64, :], den[k][:], ALU.mult,
                      [("osb", k), ("den", k)], [("ostg", oslot, bank)])
            st.dma("sp", out_rows, ostg[oslot][:], [("ostg", oslot, b) for b in range(4)], [], ("ostg", oslot))

        hs = 0
        for h in range(6):
            started = [False] * 4
            for g in range(3):
                hd = g * 6 + h
                d = A_DIL[g]
                add_head(qA[hd * 64:(hd + 1) * 64, :], kA[hd * 64:(hd + 1) * 64, :],
                         khA[hd * 64:(hd + 1) * 64, 0:d * 128], d * 128, d, Vsb, Vh, hd * 65, 0, 1, False, hs)
                hs += 1
                run_units(started)
            normalize(oA[h * 64:(h + 1) * 64, :], None, h % 2)
        for qh in range(16):
            kvh = qh // 4
            started = [False] * 4
            add_head(qB[qh * 64:(qh + 1) * 64, :], kB[kvh * 64:(kvh + 1) * 64, :],
                     khB[kvh * 64:(kvh + 1) * 64, :], 128, 1, VBs, VBh, kvh * 65, 2, 3, True, hs)
            hs += 1
            run_units(started)
            normalize(oB[qh * 64:(qh + 1) * 64, :], qh, qh % 2)
        st.emit()
        nc.all_engine_barrier()


GAMMA = [1.0 - 2.0 ** (-5.0 - h) for h in range(8)]


def stage_ret(P, l, state_only):
    nc = P.nc
    ktok = P.dram(f"ktok_C{l}", [8, TL, 128], BF16)
    VC = P.dram(f"V_C{l}", [TL, 1024], BF16)
    with nc.cleanup_on_exit():
        st = Stage(nc)
        vt = slots(st, 4, [128, 16, 128], BF16, "vt")
        ktk = slots(st, 4, [128, 16, 128], BF16, "ktk")
        S = slots(st, 4, [128, 128], F32, "S")
        Sb = slots(st, 4, [128, 128], BF16, "Sb")
        pkv_t = [st.ps([128, 512]) for _ in range(2)]
        if state_only:
            E = P.dram(f"E{l}", [8, 128, 128], F32)
        else:
            qT = P.dram(f"qT_C{l}", [1024, TL], BF16)
            kT = P.dram(f"kT_C{l}", [1024, TL], BF16)
            scg = P.dram(f"scgT{l}", [1024, TL], BF16)
            Eall = P.dram(f"Eall{l}", [NCORE, 8, 128, 128], F32)
            coef_d = P.dram("ecoef", [128, NCORE * 8], F32)
            dT_d = P.dram("decT", [128, 8, 128], F32)
            qdec_d = P.dram("qdec", [128, 8, 128], F32)
            gn_d = P.dram(f"cgn{l}", [128, 8], F32)
            id_d = P.dram("ident", [128, 128], BF16)
            oC = P.dram(f"oT_C{l}", [1024, TL], BF16)
            coef = st.sb([128, NCORE * 8], F32, "coef")
            dT = st.sb([128, 8, 128], F32, "dT")
            qdec = st.sb([128, 8, 128], F32, "qdec")
            gn = st.sb([128, 8], F32, "gn")
            ident = st.sb([128, 128], BF16, "ident")
            epst = st.sb([128, 1], F32, "eps")
            st.memset("pool", epst[:], EPS, ["eps"])
            for t, dd, kk in ((coef, coef_d, "coef"), (dT, dT_d, "dT"), (qdec, qdec_d, "qdec"), (gn, gn_d, "gn"),
                              (ident, id_d, "ident")):
                st.dma("sp", t[:], dd, [], [kk], kk)
            qs = slots(st, 4, [128, TL], BF16, "qs")
            ks = slots(st, 4, [128, TL], BF16, "ks")
            qd = slots(st, 4, [128, TL], BF16, "qd")
            sg = slots(st, 4, [128, TL], BF16, "sg")
            Ein = slots(st, 2, [128, 128], F32, "Ein")
            ps_s = [st.ps([128, 512]) for _ in range(2)]
            ps_y = [st.ps([128, 512]) for _ in range(2)]
            psT = [st.ps([128, 512]) for _ in range(2)]
            pT = slots(st, 4, [128, 128], BF16, "pT")
            junk = slots(st, 4, [128, 128], F32, "junk")
            ssq = slots(st, 4, [128, 1], F32, "ssq")
            sd = slots(st, 4, [128, 1], F32, "sd")
            rs = slots(st, 4, [128, 1], F32, "rs")
            yn = slots(st, 4, [128, 128], BF16, "yn")
            ostg = slots(st, 4, [128, TL], BF16, "ostg")
        itc = [0]

        def setup(h):
            s = h % 4
            hh = h % 2
            st.dma("sp", vt[s][:], VC[:, h * 128:(h + 1) * 128].rearrange("(b p) e -> p b e", p=128), [],
                   [("vt", s)], ("vt", s))
            st.dma("sp", ktk[s][:], ktok[h].rearrange("(b p) e -> p b e", p=128), [], [("ktk", s)], ("ktk", s))
            c = dict(h=h, s=s, hh=hh, cur=0, g128=float(GAMMA[h] ** 128))
            S0 = S[hh * 2]
            k0 = ("S", hh, 0)
            if state_only:
                st.memset("dve", S0[:], 0.0, [k0])
            else:
                rows = slice(h * 128, (h + 1) * 128)
                st.dma("sp", qs[s][:], qT[rows, :], [], [("qs", s)], ("qs", s))
                st.dma("sp", ks[s][:], kT[rows, :], [], [("ks", s)], ("ks", s))
                st.dma("sp", sg[s][:], scg[rows, :], [], [("sg", s)], ("sg", s))
                st.tt("dve", qd[s][:].rearrange("p (n i) -> p n i", i=128),
                      qs[s][:].rearrange("p (n i) -> p n i", i=128),
                      qdec[:, h, :].unsqueeze(1).broadcast_to([128, 16, 128]), ALU.mult,
                      [("qs", s), "qdec"], [("qd", s)])
                st.memset("dve", S0[:], 0.0, [k0])
                for cc in range(NCORE):
                    e2 = cc % 2
                    st.dma("sp", Ein[e2][:], Eall[cc, h], [], [("Ein", e2)], ("Ein", e2))
                    col = cc * 8 + h
                    st.stt("dve", S0[:], Ein[e2][:], coef[:, col:col + 1], S0[:], ALU.mult, ALU.add,
                           [("Ein", e2), "coef", k0], [k0])
                st.act(Sb[hh * 2][:], S0[:], AF.Copy, [k0], [("Sb", hh, 0)])
            return c

        def body_steps(c, n):
            h, s, hh, cur = c["h"], c["s"], c["hh"], c["cur"]
            it = itc[0]
            itc[0] += 1
            cs = slice(n * 128, (n + 1) * 128)
            i4 = it % 2
            i3 = it % 4
            steps = []
            if not state_only:
                steps.append(lambda: st.mm(ps_s[i4][:, 0:128], ks[s][:, cs], qs[s][:, cs], True, True,
                                           [("ks", s), ("qs", s)], [("ps_s", i4)]))
                steps.append(lambda: st.tt("dve", pT[i3][:], ps_s[i4][:, 0:128], dT[:, h, :], ALU.mult,
                                           [("ps_s", i4), "dT"], [("pT", i3)]))

                def s_y():
                    st.mm(ps_y[i4][:, 0:128], pT[i3][:], vt[s][:, n, :], True, False, [("pT", i3), ("vt", s)],
                          [("ps_y", i4)])
                    st.mm(ps_y[i4][:, 0:128], qd[s][:, cs], Sb[hh * 2 + cur][:], False, True,
                          [("qd", s), ("Sb", hh, cur)], [("ps_y", i4)])
                steps.append(s_y)
                steps.append(lambda: st.act(junk[i3][:], ps_y[i4][:, 0:128], AF.Square, [("ps_y", i4)],
                                            [("junk", i3)]))
                steps.append(lambda: st.op("dve", (lambda o, i_: (lambda e: e.reduce_sum(
                    out=o, in_=i_, axis=mybir.AxisListType.X)))(ssq[i3][:], junk[i3][:]), [("junk", i3)],
                    [("ssq", i3)]))
                steps.append(lambda: st.act(sd[i3][:], ssq[i3][:], AF.Sqrt, [("ssq", i3)], [("sd", i3)], bias=EPS,
                                            scale=1.0 / 128))
                steps.append(lambda: st.recip(rs[i3][:], sd[i3][:], [("sd", i3)], [("rs", i3)]))
                steps.append(lambda: st.ts("dve", yn[i3][:], ps_y[i4][:, 0:128], rs[i3][:], ALU.mult,
                                           [("ps_y", i4), ("rs", i3)], [("yn", i3)]))
                steps.append(lambda: st.mm(psT[i4][:, 0:128], yn[i3][:], ident[:], True, True,
                                           [("yn", i3), "ident"], [("psT", i4)]))
                steps.append(lambda: st.stt("dve", ostg[s][:, cs], psT[i4][:, 0:128], gn[:, h:h + 1], sg[s][:, cs],
                                            ALU.mult, ALU.mult, [("psT", i4), "gn", ("sg", s)], [("ostg", s, n)]))
            if n < 15 or state_only:
                k4 = it % 2
                nxt = 1 - cur
                steps.append(lambda: st.mm(pkv_t[k4][:, 0:128], ktk[s][:, n, :], vt[s][:, n, :], True, True,
                                           [("ktk", s), ("vt", s)], [("pkv", k4)]))
                steps.append(lambda: st.stt("dve", S[hh * 2 + nxt][:], S[hh * 2 + cur][:], c["g128"],
                                            pkv_t[k4][:, 0:128], ALU.mult, ALU.add, [("S", hh, cur), ("pkv", k4)],
                                            [("S", hh, nxt)]))
                if not state_only:
                    steps.append(lambda: st.act(Sb[hh * 2 + nxt][:], S[hh * 2 + nxt][:], AF.Copy, [("S", hh, nxt)],
                                                [("Sb", hh, nxt)]))
                c["cur"] = nxt
            return steps

        def finish(c):
            h, s, hh, cur = c["h"], c["s"], c["hh"], c["cur"]
            if state_only:
                st.dma("sp", E[h], S[hh * 2 + cur][:], [("S", hh, cur)], [], ("Sout", hh))
            else:
                st.dma("sp", oC[h * 128:(h + 1) * 128, :], ostg[s][:], [("ostg", s, n) for n in range(16)], [],
                       ("ostg", s))

        for p in range(4):
            ctxs = [setup(2 * p), setup(2 * p + 1)]
            for n in range(16):
                sl = [body_steps(c, n) for c in ctxs]
                for k in range(max(len(x) for x in sl)):
                    for x in sl:
                        if k < len(x):
                            x[k]()
            for c in ctxs:
                finish(c)
        st.emit()
        nc.all_engine_barrier()


def stage_merge(P, l):
    nc = P.nc
    oA = P.dram(f"oT_A{l}", [384, TL], BF16)
    oB = P.dram(f"oT_B{l}", [1024, TL], BF16)
    oC = P.dram(f"oT_C{l}", [1024, TL], BF16)
    sgd = P.dram(f"sgT{l}", [6144, TL], BF16)
    wbr = P.dram(f"w_br{l}", [16, 128, 19 * 128], F32)
    mT = P.dram(f"mT{l}", [2048, TL], BF16)
    with nc.cleanup_on_exit():
        st = Stage(nc)
        in_sb = st.sb([128, 19, TL], BF16, "oin")
        in_keys = []
        kc = 0
        for src, nch in ((oA, 3), (oB, 8), (oC, 8)):
            for c in range(nch):
                key = ("oin", kc)
                st.dma("sp", in_sb[:, kc, :], src[c * 128:(c + 1) * 128, :], [], [key], key)
                in_keys.append(key)
                kc += 1
        wst = slots(st, 3, [128, 19 * 128], F32, "wst")
        wbf = slots(st, 2, [128, 19 * 128], BF16, "wbf")
        sgs = slots(st, 2, [128, 3, TL], BF16, "sgs")
        ps = [st.ps([128, 512]) for _ in range(6)]
        m1 = slots(st, 2, [128, 512], F32, "m1")
        m2 = slots(st, 2, [128, 512], F32, "m2")
        m3 = slots(st, 2, [128, 512], F32, "m3")
        m4 = slots(st, 2, [128, 512], F32, "m4")
        stg = slots(st, 2, [128, TL], BF16, "stg")
        groups = ((0, 3), (3, 11), (11, 19))

        def load(n):
            s = n % 3
            st.dma("sp", wst[s][:], wbr[n], [], [("wst", s)], ("wst", s))

        def conv(n):
            st.copy("pool", wbf[n % 2][:], wst[n % 3][:], [("wst", n % 3)], [("wbf", n % 2)])

        def load_sg(n):
            for b3 in range(3):
                st.dma("sp", sgs[n % 2][:, b3, :], sgd[b3 * 2048 + n * 128: b3 * 2048 + (n + 1) * 128, :], [],
                       [("sgs", n % 2, b3)], ("sgs", n % 2, b3))

        load(0)
        load(1)
        load(2)
        conv(0)
        load_sg(0)
        u = 0
        for n in range(16):
            if n + 3 < 16:
                load(n + 3)
            if n + 1 < 16:
                conv(n + 1)
            s2 = n % 2
            if n + 1 < 16:
                load_sg(n + 1)
            for tb in range(4):
                cs = slice(tb * 512, (tb + 1) * 512)
                pb = (u % 2) * 3
                k = u % 2
                u += 1
                for b3, (k0, k1) in enumerate(groups):
                    for kc in range(k0, k1):
                        st.mm(ps[pb + b3][:], wbf[s2][:, kc * 128:(kc + 1) * 128], in_sb[:, kc, cs], kc == k0,
                              kc == k1 - 1, [("wbf", s2), in_keys[kc]], [("ps", pb + b3)], signal=(kc == k1 - 1))
                st.tt("dve", m1[k][:], ps[pb][:], sgs[s2][:, 0, cs], ALU.mult, [("ps", pb), ("sgs", s2, 0)],
                      [("m1", k)])
                st.tt("dve", m2[k][:], ps[pb + 1][:], sgs[s2][:, 1, cs], ALU.mult, [("ps", pb + 1), ("sgs", s2, 1)],
                      [("m2", k)])
                st.tt("dve", m3[k][:], ps[pb + 2][:], sgs[s2][:, 2, cs], ALU.mult, [("ps", pb + 2), ("sgs", s2, 2)],
                      [("m3", k)])
                st.tt("pool", m4[k][:], m1[k][:], m2[k][:], ALU.add, [("m1", k), ("m2", k)], [("m4", k)])
                st.tt("pool", stg[s2][:, cs], m4[k][:], m3[k][:], ALU.add, [("m4", k), ("m3", k)],
                      [("stg", s2, tb)])
            st.dma("sp", mT[n * 128:(n + 1) * 128, :], stg[s2][:], [("stg", s2, t) for t in range(4)], [],
                   ("stg", s2))
        st.emit()
        nc.all_engine_barrier()


def stage_wout(P, l, xin, xout):
    nc = P.nc
    mT = P.dram(f"mT{l}", [2048, TL], BF16)
    w = P.dram(f"w_out{l}", [16, 128, 2048], F32)
    xi = P.dram(xin, [2048, TL], F32)
    xo = P.dram(xout, [2048, TL], F32)
    with nc.cleanup_on_exit():
        st = Stage(nc)
        in_sb, in_keys = load_inT(st, mT, 16)
        linF(st, in_sb, in_keys, 16, [(w, 16, h_resid(st, xi, xo, "xres"))])
        st.emit()
        nc.all_engine_barrier()


def stage_mlp(P, l, uname, xin, xout):
    nc = P.nc
    u2 = P.dram(uname, [2048, TL], BF16)
    wu = P.dram(f"w_up{l}", [64, 128, 2048], F32)
    wd = P.dram(f"w_dn{l}", [4, 16, 128, 2048], F32)
    xi = P.dram(xin, [2048, TL], F32)
    xo = P.dram(xout, [2048, TL], F32)
    with nc.cleanup_on_exit():
        st = Stage(nc)
        in_sb, in_keys = load_inT(st, u2, 16)
        hq = st.sb([128, 16, TL], BF16, "hq")
        wst = slots(st, 3, [128, 2048], F32, "wst")
        wbf = slots(st, 2, [128, 2048], BF16, "wbf")
        ps = [st.ps([128, 512]) for _ in range(6)]
        hu = h_relu2_sb(st, hq, "hq")
        hd0 = h_resid(st, xi, xo, "xres")
        hd = h_resid(st, xo, xo, "xres") if xin != xout else hd0
        seq = []
        for q in range(4):
            for n in range(16):
                seq.append(("u", q, n))
            for n in range(16):
                seq.append(("d", q, n))

        def load(i):
            kind, q, n = seq[i]
            s = i % 3
            src = wu[q * 16 + n] if kind == "u" else wd[q, n]
            st.dma("sp", wst[s][:], src, [], [("wst", s)], ("wst", s))

        def conv(i):
            st.copy("pool", wbf[i % 2][:], wst[i % 3][:], [("wst", i % 3)], [("wbf", i % 2)])

        load(0)
        load(1)
        load(2)
        conv(0)
        u = 0
        for i, (kind, q, n) in enumerate(seq):
            if i + 3 < len(seq):
                load(i + 3)
            if i + 1 < len(seq):
                conv(i + 1)
            s2 = i % 2
            for tb in range(4):
                b = u % 6
                u += 1
                cs = slice(tb * 512, (tb + 1) * 512)
                for kc in range(16):
                    if kind == "u":
                        rhs, rk = in_sb[:, kc, cs], [in_keys[kc]]
                    else:
                        rhs, rk = hq[:, kc, cs], [("hq", kc, tb)]
                    st.mm(ps[b][:], wbf[s2][:, kc * 128:(kc + 1) * 128], rhs, kc == 0, kc == 15,
                          [("wbf", s2)] + rk, [("ps", b)], signal=(kc == 15))
                if kind == "u":
                    hu(n, tb, ps[b], ("ps", b))
                else:
                    (hd0 if q == 0 else hd)(n, tb, ps[b], ("ps", b))
        st.emit()
        nc.all_engine_barrier()


IN_SIZES = (1152, 1152, 1152, 1024, 256, 256, 1024, 1024, 1024, 1024, 2048, 2048, 2048)
OFF = np.concatenate([[0], np.cumsum(IN_SIZES)])


def tile_w(w):
    K, N = w.shape
    return np.ascontiguousarray(
        w.reshape(K // 128, 128, N // 128, 128).transpose(2, 1, 0, 3).reshape(N // 128, 128, K))


def tile_wT(w):
    K, N = w.shape
    return np.ascontiguousarray(w.reshape(K // 128, 128, N).transpose(1, 0, 2))


def rep128(v):
    return np.ascontiguousarray(np.broadcast_to(np.asarray(v, np.float32)[None, :], (128, len(v))))


def core_consts(c):
    f32 = np.float32
    pos = (np.arange(TL) + c * TL).astype(f32)
    out = {}
    inv = (f32(500000.0) ** (-(np.arange(8, dtype=f32) / f32(8)))).astype(f32)
    ang = (pos[None, :] * inv[:, None]).astype(f32)
    cos_ab = np.ones((128, TL), f32)
    sin_ab = np.zeros((128, TL), f32)
    rot_ab = np.zeros((128, 128), f32)
    for p in range(128):
        i = p % 64
        if i < 16:
            cos_ab[p] = np.cos(ang[i % 8])
            sin_ab[p] = np.sin(ang[i % 8])
            if i < 8:
                rot_ab[p + 8, p] = -1.0
            else:
                rot_ab[p - 8, p] = 1.0
    out["cos_ab"], out["sin_ab"], out["rot_ab"] = cos_ab, sin_ab, rot_ab.astype(NPBF)
    invc = (f32(10000.0) ** (-(np.arange(64, dtype=f32) / f32(64)))).astype(f32)
    angc = (pos[None, :] * invc[:, None]).astype(f32)
    out["cos_c"] = np.cos(angc)[np.arange(128) % 64].astype(f32)
    out["sin_c"] = np.sin(angc)[np.arange(128) % 64].astype(f32)
    rot_c = np.zeros((128, 128), f32)
    for p in range(128):
        if p < 64:
            rot_c[p + 64, p] = -1.0
        else:
            rot_c[p - 64, p] = 1.0
    out["rot_c"] = rot_c.astype(NPBF)
    bones = np.zeros((128, 128), f32)
    bones[:64, :64] = 1.0
    bones[64:, 64:] = 1.0
    out["bones"] = bones
    out["ident"] = np.eye(128, dtype=f32).astype(NPBF)
    k = np.arange(128)[:, None]
    q = np.arange(128)[None, :]
    own = (k <= q).astype(f32)
    prevA = (k >= q).astype(f32)
    prevB = (k > q).astype(f32)
    hv = 1.0 if c > 0 else 0.0
    masks = np.zeros((128, 4, 256), f32)
    masks[:, 0] = np.concatenate([prevA, own], 1)
    masks[:, 1] = np.concatenate([prevA * hv, own], 1)
    masks[:, 2] = np.concatenate([prevB, own], 1)
    masks[:, 3] = np.concatenate([prevB * hv, own], 1)
    out["masks"] = masks.astype(NPBF)
    sel = np.zeros((65, 64), f32)
    sel[64, :] = 1.0
    out["sel65"] = sel
    lg = np.log1p(-(2.0 ** (-5.0 - np.arange(8, dtype=np.float64))))
    i = np.arange(128, dtype=np.float64)
    decT = np.zeros((128, 8, 128), np.float64)
    for h in range(8):
        rel = i[None, :] - i[:, None]
        decT[:, h, :] = np.where(rel >= 0, np.exp(lg[h] * np.maximum(rel, 0)), 0.0) / np.sqrt(128.0)
    out["decT"] = decT.astype(f32)
    qdec = np.exp(lg[None, :, None] * (i[None, None, :] + 1.0))
    out["qdec"] = np.ascontiguousarray(np.broadcast_to(qdec, (128, 8, 128))).astype(f32)
    kdec = np.exp(lg[None, :] * (127.0 - i[:, None])) / np.sqrt(128.0)
    out["kdec"] = kdec.astype(f32)
    coef = np.zeros((NCORE, 8), np.float64)
    for cp in range(c):
        coef[cp] = np.exp(lg * (TL * (c - 1 - cp)))
    out["ecoef"] = rep128(coef.reshape(-1))
    return out


def layer_weights(inputs, l):
    w_in = np.asarray(inputs["w_in"][l], np.float32)
    col = lambda i: w_in[:, OFF[i]:OFF[i + 1]]
    o = {}
    o[f"wq_a{l}"] = tile_w(col(0))
    o[f"wk_a{l}"] = tile_w(col(1))
    o[f"wv_a{l}"] = tile_wT(col(2))
    o[f"wq_b{l}"] = tile_w(col(3))
    o[f"wk_b{l}"] = tile_w(col(4))
    o[f"wv_b{l}"] = tile_wT(col(5))
    o[f"wq_c{l}"] = tile_w(col(6))
    o[f"wk_c{l}"] = tile_w(col(7))
    o[f"wv_c{l}"] = tile_wT(col(8))
    o[f"wg_c{l}"] = tile_w(col(9))
    o[f"wg_m{l}"] = tile_w(w_in[:, OFF[10]:OFF[13]])
    wbr = np.concatenate([inputs["w_br_a"][l], inputs["w_br_b"][l], inputs["w_br_c"][l]], 0).astype(np.float32)
    o[f"w_br{l}"] = tile_w(wbr)
    o[f"w_out{l}"] = tile_w(np.asarray(inputs["w_out"][l], np.float32))
    o[f"w_up{l}"] = tile_w(np.asarray(inputs["w_up"][l], np.float32))
    wd = np.asarray(inputs["w_down"][l], np.float32)
    o[f"w_dn{l}"] = np.stack([tile_w(wd[q * 2048:(q + 1) * 2048]) for q in range(4)], 0)
    o[f"g_mix{l}"] = np.ascontiguousarray(np.asarray(inputs["mix_norm"][l], np.float32).reshape(16, 128).T)
    o[f"g_mlp{l}"] = np.ascontiguousarray(np.asarray(inputs["mlp_norm"][l], np.float32).reshape(16, 128).T)
    for nm, key in (("gq_a", "a_q_norm"), ("gk_a", "a_k_norm"), ("gq_b", "b_q_norm"), ("gk_b", "b_k_norm")):
        v = np.asarray(inputs[key][l], np.float32)
        o[f"{nm}{l}"] = np.ascontiguousarray(np.tile(v, 2).reshape(128, 1))
    o[f"sinks{l}"] = rep128(inputs["b_sinks"][l])
    o[f"cgn{l}"] = np.ascontiguousarray(np.asarray(inputs["c_gn"][l], np.float32).reshape(8, 128).T)
    return o


def build_launch(stages, ext_in, ext_out):
    P = Prog(ext_in, ext_out)
    for fn in stages:
        fn(P)
    return P


def phase1_stages(l, xname):
    return [
        lambda P: stage_norm(P, xname, f"g_mix{l}", f"uT{l}"),
        lambda P: stage_qk_ab(P, l, "k"),
        lambda P: stage_c_rot(P, l, "k"),
        lambda P: stage_v(P, l),
        lambda P: stage_ret(P, l, True),
    ]


def phase2_stages(l, xname, x1name, x2name):
    return [
        lambda P: stage_qk_ab(P, l, "q"),
        lambda P: stage_c_rot(P, l, "q"),
        lambda P: stage_gates(P, l),
        lambda P: stage_attn(P, l),
        lambda P: stage_ret(P, l, False),
        lambda P: stage_merge(P, l),
        lambda P: stage_wout(P, l, xname, x1name),
        lambda P: stage_norm(P, x1name, f"g_mlp{l}", f"u2T{l}"),
        lambda P: stage_mlp(P, l, f"u2T{l}", x1name, x1name),
    ]


def run_launch(P, per_core_inputs, ncores):
    in_maps = []
    for c in range(ncores):
        m = {}
        for name, (shape, dt) in P.used_in.items():
            a = per_core_inputs[c][name]
            assert tuple(a.shape) == tuple(shape), (name, a.shape, shape)
            m[name] = a
        in_maps.append(m)
    res = run_bass_kernel_spmd(P.nc, in_maps, core_ids=list(range(ncores)))
    return res.results


def halo_from(prev, l):
    h = {}
    kA = prev[f"kT_A{l}"]
    VA = prev[f"V_A{l}"]
    khA = np.zeros_like(kA)
    vhA = np.zeros_like(VA)
    for g, d in enumerate(A_DIL):
        L = TL // d
        for r in range(d):
            khA[g * 384:(g + 1) * 384, r * 128:(r + 1) * 128] = kA[g * 384:(g + 1) * 384, (r + 1) * L - 128:(r + 1) * L]
            vhA[r * 128:(r + 1) * 128, g * 390:(g + 1) * 390] = VA[(r + 1) * L - 128:(r + 1) * L, g * 390:(g + 1) * 390]
    h[f"khalo_A{l}"] = khA
    h[f"vhalo_A{l}"] = vhA
    h[f"khalo_B{l}"] = np.ascontiguousarray(prev[f"kT_B{l}"][:, TL - 128:])
    h[f"vhalo_B{l}"] = np.ascontiguousarray(prev[f"V_B{l}"][TL - 128:, :])
    return h


P1_OUT = lambda l: [f"uT{l}", f"kT_A{l}", f"kT_B{l}", f"kT_C{l}", f"ktok_C{l}", f"V_A{l}", f"V_B{l}", f"V_C{l}", f"E{l}"]


def kernel(**inputs):
    x = np.asarray(inputs["x"], np.float32)[0]
    ncores = NCORE
    consts = [core_consts(c) for c in range(ncores)]
    xT = [np.ascontiguousarray(x[c * TL:(c + 1) * TL].T) for c in range(ncores)]
    cur = xT
    for l in range(2):
        lw = layer_weights(inputs, l)
        xname = f"xin{l}"
        P1 = build_launch(phase1_stages(l, xname), ext_in=set(lw) | set(consts[0]) | {xname}, ext_out=set(P1_OUT(l)))
        pci = [dict(lw, **consts[c], **{xname: cur[c]}) for c in range(ncores)]
        r1 = run_launch(P1, pci, ncores)
        Eall = np.stack([r1[c][f"E{l}"] for c in range(ncores)], 0)
        zero_prev = {k: np.zeros_like(v) for k, v in r1[0].items()}
        pci2 = []
        for c in range(ncores):
            d = dict(lw, **consts[c])
            d.update({k: r1[c][k] for k in P1_OUT(l)})
            d.update(halo_from(r1[c - 1] if c > 0 else zero_prev, l))
            d[f"Eall{l}"] = Eall
            d[xname] = cur[c]
            pci2.append(d)
        x1 = f"xmid{l}"
        P2 = build_launch(phase2_stages(l, xname, x1, x1), ext_in=set(pci2[0]), ext_out={x1})
        r2 = run_launch(P2, pci2, ncores)
        cur = [r2[c][x1] for c in range(ncores)]
    out = np.concatenate([cur[c].T for c in range(ncores)], 0)[None]
    return np.ascontiguousarray(out.astype(np.float32))
```
